# Optimizing a Trainium2 kernel written in Bass

```python
import math
import jax, jax.numpy as jnp
from jax import lax
import numpy as np

D_MODEL = 1024
BATCH = 4
SEQ = 4096
DEPTH = 2

RET_HEADS = 4
RET_HEAD_DIM = 128
RET_CHUNK = 128
DSA_HEADS = 8
DSA_HEAD_DIM = 64
DSA_PATTERNS = ((128, 1), (512, 4), (2048, 16))
DSA_BLOCK = 128
T5_BUCKETS = 32
T5_MAX_DIST = 2048
MLSTM_HEADS = 4
MLSTM_HEAD_DIM = 256
MLSTM_CHUNK = 128
MLSTM_CONV = 4
D_FF = 2816
FFN_CONV = 3
LN_EPS = 1e-5
DEEPNORM_ALPHA = (2.0 * DEPTH) ** 0.25
DEEPNORM_BETA = (8.0 * DEPTH) ** -0.25

N_EVEN = (DEPTH + 1) // 2
N_ODD = DEPTH // 2
RET_W = RET_HEADS * RET_HEAD_DIM
DSA_W = DSA_HEADS * DSA_HEAD_DIM
MIX_W_EVEN = RET_W + DSA_W
EVEN_IN = 4 * RET_W + 3 * DSA_W
EVEN_SPLITS = (RET_W, 2 * RET_W, 3 * RET_W, 4 * RET_W, 4 * RET_W + DSA_W, 4 * RET_W + 2 * DSA_W)
MLSTM_W = MLSTM_HEADS * MLSTM_HEAD_DIM
ODD_IN = 4 * MLSTM_W + 2 * MLSTM_HEADS
ODD_SPLITS = (2 * MLSTM_W, 3 * MLSTM_W, 4 * MLSTM_W)

kernel_name = "hybrid_retention_dilated_mlstm_deepnorm"


def layer_norm(x, g, b):
    xf = x.astype(jnp.float32)
    mu = jnp.mean(xf, -1, keepdims=True)
    var = jnp.mean(jnp.square(xf - mu), -1, keepdims=True)
    return ((xf - mu) * lax.rsqrt(var + LN_EPS) * g + b).astype(x.dtype)


def head_norm(y):
    mu = jnp.mean(y, -1, keepdims=True)
    var = jnp.mean(jnp.square(y - mu), -1, keepdims=True)
    return (y - mu) * lax.rsqrt(var + LN_EPS)


def split_heads(t, n_heads):
    B, S, W = t.shape
    return t.reshape(B, S, n_heads, W // n_heads).transpose(0, 2, 1, 3)


def merge_heads(t):
    B, H, S, d = t.shape
    return t.transpose(0, 2, 1, 3).reshape(B, S, H * d)


def causal_dwconv(x, w):
    K, C = w.shape
    return lax.conv_general_dilated(x, w[:, None, :].astype(x.dtype), window_strides=(1,), padding=[(K - 1, 0)], dimension_numbers=("NWC", "WIO", "NWC"), feature_group_count=C)


def rotary(x):
    S, d = x.shape[-2], x.shape[-1]
    inv = 1.0 / (10000.0 ** (jnp.arange(0, d, 2, dtype=jnp.float32) / d))
    ang = jnp.arange(S, dtype=jnp.float32)[:, None] * inv[None, :]
    cos, sin = jnp.cos(ang), jnp.sin(ang)
    x1, x2 = x[..., : d // 2], x[..., d // 2:]
    return jnp.concatenate([x1 * cos - x2 * sin, x1 * sin + x2 * cos], -1)


def retention(q, k, v):
    B, H, S, d = q.shape
    C = RET_CHUNK
    N = S // C
    log_gamma = jnp.log1p(-jnp.exp2(-5.0 - jnp.arange(H, dtype=jnp.float32)))
    qf = rotary(q.astype(jnp.float32))
    kf = rotary(k.astype(jnp.float32)) * (d ** -0.5)
    qc = qf.reshape(B, H, N, C, d)
    kc = kf.reshape(B, H, N, C, d)
    vc = v.astype(jnp.float32).reshape(B, H, N, C, d)
    idx = jnp.arange(C, dtype=jnp.float32)
    rel = idx[:, None] - idx[None, :]
    decay = jnp.where(rel >= 0, jnp.exp(log_gamma[:, None, None] * jnp.maximum(rel, 0.0)), 0.0)
    scores = jnp.einsum('bhncd,bhnsd->bhncs', qc, kc) * decay[None, :, None]
    y_intra = jnp.einsum('bhncs,bhnse->bhnce', scores, vc)
    k_w = jnp.exp(log_gamma[:, None] * (C - 1 - idx)[None, :])
    kv = jnp.einsum('bhnsd,bhnse->bhnde', kc * k_w[None, :, None, :, None], vc)
    chunk_decay = jnp.exp(log_gamma * C)[None, :, None, None]

    def step(R, kv_n):
        return chunk_decay * R + kv_n, R

    _, R_prev = lax.scan(step, jnp.zeros((B, H, d, d), jnp.float32), jnp.moveaxis(kv, 2, 0))
    R_prev = jnp.moveaxis(R_prev, 0, 2)
    q_w = jnp.exp(log_gamma[:, None] * (idx + 1.0)[None, :])
    y_inter = jnp.einsum('bhncd,bhnde->bhnce', qc * q_w[None, :, None, :, None], R_prev)
    return (y_intra + y_inter).reshape(B, H, S, d)


def t5_bucket(dist):
    exact = T5_BUCKETS // 2
    n = jnp.maximum(dist, 0)
    large = exact + (jnp.log(jnp.maximum(n, 1).astype(jnp.float32) / exact) / math.log(T5_MAX_DIST / exact) * (T5_BUCKETS - exact)).astype(jnp.int32)
    large = jnp.minimum(large, T5_BUCKETS - 1)
    return jnp.where(n < exact, n, large)


def dilated_branch(q, k, v, rel_bias, window, dilation):
    B, H, S, d = q.shape
    L = S // dilation
    W = window // dilation
    blk = DSA_BLOCK
    nb = -(-L // blk)
    Lp = nb * blk

    def to_blocks(t):
        t = t.astype(jnp.float32).reshape(B, H, L, dilation, d).swapaxes(2, 3)
        t = jnp.pad(t, ((0, 0), (0, 0), (0, 0), (0, Lp - L), (0, 0)))
        return t.reshape(B, H, dilation, nb, blk, d)

    def with_prev(t):
        prev = jnp.pad(t[:, :, :, :-1], ((0, 0), (0, 0), (0, 0), (1, 0), (0, 0), (0, 0)))
        return jnp.concatenate([prev, t], axis=4)

    qb = to_blocks(q)
    kw = with_prev(to_blocks(k))
    vw = with_prev(to_blocks(v))
    qi = jnp.arange(blk)[:, None] + blk
    kj = jnp.arange(2 * blk)[None, :]
    dist = qi - kj
    bias = rel_bias[t5_bucket(dist * dilation)].astype(jnp.float32).transpose(2, 0, 1)
    blk_start = jnp.arange(nb)[:, None, None] * blk
    valid = (dist >= 0) & (dist <= W) & (blk_start + kj - blk >= 0)
    logits = jnp.einsum('bhrnqd,bhrnkd->bhrnqk', qb, kw) * (d ** -0.5) + bias[None, :, None, None]
    logits = jnp.where(valid, logits, -jnp.inf)
    m = jnp.max(logits, -1)
    p = jnp.exp(logits - m[..., None])
    s = jnp.sum(p, -1)
    o = jnp.einsum('bhrnqk,bhrnkd->bhrnqd', p, vw)

    def from_blocks(t):
        tail = t.shape[5:]
        t = t.reshape((B, H, dilation, Lp) + tail)[:, :, :, :L]
        return jnp.swapaxes(t, 2, 3).reshape((B, H, S) + tail)

    return from_blocks(m), from_blocks(s), from_blocks(o)


def dilated_attention(q, k, v, rel_bias):
    outs = [dilated_branch(q, k, v, rel_bias, w, r) for (w, r) in DSA_PATTERNS]
    M = jnp.max(jnp.stack([m for (m, _, _) in outs]), axis=0)
    num = sum(jnp.exp(m - M)[..., None] * o for (m, _, o) in outs)
    den = sum(jnp.exp(m - M) * s for (m, s, _) in outs)
    return num / den[..., None]


def mlstm(q, k, v, log_i, log_f):
    B, H, S, d = q.shape
    C = MLSTM_CHUNK
    N = S // C
    qc = q.astype(jnp.float32).reshape(B, H, N, C, d)
    kc = (k.astype(jnp.float32) * (d ** -0.5)).reshape(B, H, N, C, d)
    vc = v.astype(jnp.float32).reshape(B, H, N, C, d)
    li = log_i.reshape(B, H, N, C)
    a = jnp.cumsum(log_f.reshape(B, H, N, C), axis=-1)
    g = a[..., -1]
    w_state = g[..., None] - a + li
    m_loc = jnp.max(w_state, -1)
    ew = jnp.exp(w_state - m_loc[..., None])[..., None]
    kv = jnp.einsum('bhncd,bhnce->bhnde', kc * ew, vc)
    ksum = jnp.sum(kc * ew, axis=3)

    def step(carry, inp):
        C_s, n_s, m_s = carry
        kv_n, ks_n, g_n, m_n = inp
        m_new = jnp.maximum(g_n + m_s, m_n)
        a_old = jnp.exp(g_n + m_s - m_new)
        a_loc = jnp.exp(m_n - m_new)
        C_new = a_old[..., None, None] * C_s + a_loc[..., None, None] * kv_n
        n_new = a_old[..., None] * n_s + a_loc[..., None] * ks_n
        return (C_new, n_new, m_new), (C_s, n_s, m_s)

    init = (jnp.zeros((B, H, d, d), jnp.float32), jnp.zeros((B, H, d), jnp.float32), jnp.zeros((B, H), jnp.float32))
    xs = (jnp.moveaxis(kv, 2, 0), jnp.moveaxis(ksum, 2, 0), jnp.moveaxis(g, 2, 0), jnp.moveaxis(m_loc, 2, 0))
    _, (C_prev, n_prev, m_prev) = lax.scan(step, init, xs)
    C_prev = jnp.moveaxis(C_prev, 0, 2)
    n_prev = jnp.moveaxis(n_prev, 0, 2)
    m_prev = jnp.moveaxis(m_prev, 0, 2)
    D = a[..., :, None] - a[..., None, :] + li[..., None, :]
    causal = jnp.tril(jnp.ones((C, C), dtype=bool))
    D = jnp.where(causal, D, -jnp.inf)
    inter_log = a + m_prev[..., None]
    m = jnp.maximum(inter_log, jnp.max(D, -1))
    P = jnp.exp(D - m[..., None]) * jnp.einsum('bhncd,bhnsd->bhncs', qc, kc)
    w_inter = jnp.exp(inter_log - m)
    num = jnp.einsum('bhncs,bhnse->bhnce', P, vc) + w_inter[..., None] * jnp.einsum('bhncd,bhnde->bhnce', qc, C_prev)
    den = jnp.sum(P, -1) + w_inter * jnp.einsum('bhncd,bhnd->bhnc', qc, n_prev)
    h = num / jnp.maximum(jnp.abs(den), jnp.exp(-m))[..., None]
    return h.reshape(B, H, S, d)


def even_mixer(x, w_in, w_out, rel_bias):
    h = x @ w_in
    q_r, k_r, v_r, g_r, q_d, k_d, v_d = jnp.split(h, list(EVEN_SPLITS), axis=-1)
    y_r = retention(split_heads(q_r, RET_HEADS), split_heads(k_r, RET_HEADS), split_heads(v_r, RET_HEADS))
    y_r = merge_heads(head_norm(y_r)) * jax.nn.silu(g_r.astype(jnp.float32))
    y_d = merge_heads(dilated_attention(split_heads(q_d, DSA_HEADS), split_heads(k_d, DSA_HEADS), split_heads(v_d, DSA_HEADS), rel_bias))
    y = jnp.concatenate([y_r, y_d], axis=-1).astype(x.dtype)
    return y @ w_out


def odd_mixer(x, w_in, gate_b, conv_w, w_out):
    h = x @ w_in
    qk, v, o, gates = jnp.split(h, list(ODD_SPLITS), axis=-1)
    qk = jax.nn.silu(causal_dwconv(qk, conv_w))
    q, k = jnp.split(qk, 2, axis=-1)
    gates = gates.astype(jnp.float32) + gate_b
    log_i = gates[..., :MLSTM_HEADS].transpose(0, 2, 1)
    log_f = jax.nn.log_sigmoid(gates[..., MLSTM_HEADS:]).transpose(0, 2, 1)
    y = mlstm(split_heads(q, MLSTM_HEADS), split_heads(k, MLSTM_HEADS), split_heads(v, MLSTM_HEADS), log_i, log_f)
    y = merge_heads(y) * jax.nn.sigmoid(o.astype(jnp.float32))
    return y.astype(x.dtype) @ w_out


def conv_ffn(x, w_up, conv_w, conv_b, w_down):
    gate, up = jnp.split(x @ w_up, 2, axis=-1)
    act = jax.nn.silu(causal_dwconv(gate, conv_w) + conv_b)
    return (act * up) @ w_down


def setup_inputs(seed: int = 0) -> dict:
    key = jax.random.key(seed)
    ks = jax.random.split(key, 16)
    f32 = jnp.float32

    def nrm(k, shape, scale):
        return jax.random.normal(k, shape, f32) * scale

    d_in = D_MODEL ** -0.5
    x = nrm(ks[0], (BATCH, SEQ, D_MODEL), 1.0)
    even_scale = np.ones(EVEN_IN, np.float32)
    even_scale[2 * RET_W:3 * RET_W] = DEEPNORM_BETA
    even_scale[4 * RET_W + 2 * DSA_W:] = DEEPNORM_BETA
    even_w_in = nrm(ks[1], (N_EVEN, D_MODEL, EVEN_IN), d_in) * jnp.asarray(even_scale)
    even_w_out = nrm(ks[2], (N_EVEN, MIX_W_EVEN, D_MODEL), MIX_W_EVEN ** -0.5 * DEEPNORM_BETA)
    rel_bias = nrm(ks[3], (T5_BUCKETS, DSA_HEADS), 0.5)
    odd_scale = np.ones(ODD_IN, np.float32)
    odd_scale[2 * MLSTM_W:3 * MLSTM_W] = DEEPNORM_BETA
    odd_w_in = nrm(ks[4], (N_ODD, D_MODEL, ODD_IN), d_in) * jnp.asarray(odd_scale)
    i_bias = nrm(ks[5], (N_ODD, MLSTM_HEADS), 0.1)
    f_bias = jnp.linspace(3.0, 6.0, MLSTM_HEADS, dtype=f32) + nrm(ks[6], (N_ODD, MLSTM_HEADS), 0.1)
    odd_gate_b = jnp.concatenate([i_bias, f_bias], axis=-1)
    odd_conv_w = nrm(ks[7], (N_ODD, MLSTM_CONV, 2 * MLSTM_W), MLSTM_CONV ** -0.5)
    odd_w_out = nrm(ks[8], (N_ODD, MLSTM_W, D_MODEL), MLSTM_W ** -0.5 * DEEPNORM_BETA)
    ffn_w_up = nrm(ks[9], (DEPTH, D_MODEL, 2 * D_FF), d_in * DEEPNORM_BETA)
    ffn_conv_w = nrm(ks[10], (DEPTH, FFN_CONV, D_FF), FFN_CONV ** -0.5)
    ffn_conv_b = nrm(ks[11], (DEPTH, D_FF), 0.02)
    ffn_w_down = nrm(ks[12], (DEPTH, D_FF, D_MODEL), D_FF ** -0.5 * DEEPNORM_BETA)
    ln_g = 1.0 + nrm(ks[13], (DEPTH, 2, D_MODEL), 0.02)
    ln_b = nrm(ks[14], (DEPTH, 2, D_MODEL), 0.02)
    return {"x": x, "even_w_in": even_w_in, "even_w_out": even_w_out, "rel_bias": rel_bias, "odd_w_in": odd_w_in, "odd_gate_b": odd_gate_b, "odd_conv_w": odd_conv_w, "odd_w_out": odd_w_out, "ffn_w_up": ffn_w_up, "ffn_conv_w": ffn_conv_w, "ffn_conv_b": ffn_conv_b, "ffn_w_down": ffn_w_down, "ln_g": ln_g, "ln_b": ln_b}


def reference(x, even_w_in, even_w_out, rel_bias, odd_w_in, odd_gate_b, odd_conv_w, odd_w_out, ffn_w_up, ffn_conv_w, ffn_conv_b, ffn_w_down, ln_g, ln_b):
    for layer in range(DEPTH):
        j = layer // 2
        if layer % 2 == 0:
            mix = even_mixer(x, even_w_in[j], even_w_out[j], rel_bias)
        else:
            mix = odd_mixer(x, odd_w_in[j], odd_gate_b[j], odd_conv_w[j], odd_w_out[j])
        x = layer_norm(DEEPNORM_ALPHA * x + mix, ln_g[layer, 0], ln_b[layer, 0])
        ffn = conv_ffn(x, ffn_w_up[layer], ffn_conv_w[layer], ffn_conv_b[layer], ffn_w_down[layer])
        x = layer_norm(DEEPNORM_ALPHA * x + ffn, ln_g[layer, 1], ln_b[layer, 1])
    return x
```

```python
import math, os
from contextlib import ExitStack
import numpy as np
import concourse.bass as bass
import concourse.mybir as mybir
from concourse.bass_utils import run_bass_kernel_spmd

F32 = mybir.dt.float32
BF16 = mybir.dt.bfloat16
AF = mybir.ActivationFunctionType
ALU = mybir.AluOpType
AX = mybir.AxisListType

S_LEN = 4096
D = 1024
NT = S_LEN // 128
NG = S_LEN // 512
D_FF = 2816
NFC = D_FF // 128
ALPHA = 4.0 ** 0.25
LN_EPS = 1e-5

COMPUTE = ("pe", "act", "dve", "pool")
QUEUES = ("sp", "act", "pool")
NSLOT = 8


class Buf:
    __slots__ = ("name", "w", "r", "rd")

    def __init__(self, name=""):
        self.name = name
        self.w = None
        self.r = {}
        self.rd = []


def bufs(name, n):
    return [Buf(f"{name}{i}") for i in range(n)]


class Op:
    __slots__ = ("eng", "fn", "deps", "dma", "sig", "val", "sem", "k")

    def __init__(self, eng, fn, dma):
        self.eng = eng
        self.fn = fn
        self.dma = dma
        self.deps = []
        self.sig = False
        self.val = 0
        self.sem = None
        self.k = 0


class Sched:
    uid = 0

    def __init__(self):
        self.ops = {e: [] for e in ("pe", "act", "dve", "pool", "sp")}

    def op(self, eng, fn, reads=(), writes=(), dma=False):
        o = Op(eng, fn, dma)
        deps = {}

        def add(p, raw):
            if p is None:
                return
            if not p.dma and not dma and p.eng == eng:
                if not raw or eng == "pe":
                    return
            deps[id(p)] = p

        for b in reads:
            add(b.w, True)
        for b in writes:
            add(b.w, False)
            for p in b.r.values():
                add(p, False)
            for p in b.rd:
                add(p, False)
        o.deps = list(deps.values())
        for p in o.deps:
            p.sig = True
        ws = set(id(b) for b in writes)
        for b in writes:
            b.w = o
            b.r = {}
            b.rd = []
        for b in reads:
            if id(b) in ws:
                continue
            if dma:
                b.rd.append(o)
            else:
                b.r[eng] = o
        self.ops[eng].append(o)
        return o

    def emit(self, nc, es):
        Sched.uid += 1
        u = Sched.uid
        csem = {e: nc.alloc_semaphore(name=f"c{u}_{e}") for e in COMPUTE}
        qsem = {q: [nc.alloc_semaphore(name=f"q{u}_{q}{i}") for i in range(NSLOT)] for q in QUEUES}
        self.sems = list(csem.values()) + [s_ for q in QUEUES for s_ in qsem[q]]
        for e in COMPUTE:
            c = 0
            for o in self.ops[e]:
                if o.dma:
                    continue
                if o.sig:
                    c += 1
                    o.val = c
                    o.sem = csem[e]
        dmas = {q: [] for q in QUEUES}
        for q in QUEUES:
            k = 0
            for o in self.ops[q]:
                if not o.dma:
                    continue
                o.k = k
                o.sem = qsem[q][k % NSLOT]
                o.val = 16 * (k // NSLOT + 1)
                dmas[q].append(o)
                k += 1
        ops = self.ops

        def run(eng_name, eng):
            waited = {}

            def wait(sem, val):
                key = id(sem)
                if waited.get(key, 0) >= val:
                    return
                waited[key] = val
                eng.wait_ge(sem, val)

            for o in ops[eng_name]:
                for p in o.deps:
                    wait(p.sem, p.val)
                if o.dma and o.k >= NSLOT:
                    prev = dmas[eng_name][o.k - NSLOT]
                    wait(prev.sem, prev.val)
                ins = o.fn(eng)
                if o.dma:
                    ins.then_inc(o.sem, 16)
                elif o.sig:
                    ins.then_inc(o.sem, 1)
            if eng_name in dmas:
                for o in dmas[eng_name][-NSLOT:]:
                    wait(o.sem, o.val)

        with nc.Block() as block:
            @block.tensor
            def _(e):
                run("pe", e)

            @block.scalar
            def _(e):
                run("act", e)

            @block.vector
            def _(e):
                run("dve", e)

            @block.gpsimd
            def _(e):
                run("pool", e)

            @block.sync
            def _(e):
                run("sp", e)


class Rot:
    def __init__(self, tiles):
        self.t = tiles
        self.b = bufs("rot", len(tiles))
        self.i = 0

    def next(self):
        i = self.i % len(self.t)
        self.i += 1
        return self.t[i], self.b[i]


class P:
    def __init__(self, nc, name):
        self.nc = nc
        self.name = name
        self.es = ExitStack()
        self.S = Sched()
        self.n = 0

    def sb(self, shape, dt, name=None):
        self.n += 1
        return self.es.enter_context(self.nc.sbuf_tensor(f"{self.name}_{name or 't'}{self.n}", list(shape), dt))

    def ps(self, shape, dt, name=None):
        self.n += 1
        return self.es.enter_context(self.nc.psum_tensor(f"{self.name}_{name or 'p'}{self.n}", list(shape), dt))

    def rot_sb(self, n, shape, dt, name=None):
        return Rot([self.sb(shape, dt, name) for _ in range(n)])

    def rot_ps(self, n, shape, dt, name=None):
        return Rot([self.ps(shape, dt, name) for _ in range(n)])

    def finish(self):
        self.S.emit(self.nc, self.es)
        self.nc.all_engine_barrier()
        self.nc.clear_and_free_semaphores(self.S.sems)
        self.nc.all_engine_barrier()
        self.es.close()


def dma(S, q, out, in_, reads=(), writes=()):
    return S.op(q, lambda e: e.dma_start(out=out, in_=in_), reads=reads, writes=writes, dma=True)


def load_weight(p, dst, dst_bufs, src, nchunk, cols, stage, cast_engs=("pool",), stage_cols=3584):
    S = p.S
    i = 0
    for k in range(nchunk):
        c0 = 0
        while c0 < cols:
            cw = min(stage_cols, cols - c0)
            st, sbuf = stage.next()
            dma(S, "sp", st[:, 0:cw], src[k * 128:(k + 1) * 128, c0:c0 + cw], writes=[sbuf])
            eng = cast_engs[i % len(cast_engs)]
            i += 1
            if eng == "act":
                S.op("act", lambda e, st=st, k=k, c0=c0, cw=cw: e.copy(out=dst[:, k, c0:c0 + cw], in_=st[:, 0:cw]),
                     reads=[sbuf], writes=[dst_bufs[k]])
            else:
                S.op(eng, lambda e, st=st, k=k, c0=c0, cw=cw: e.tensor_copy(out=dst[:, k, c0:c0 + cw], in_=st[:, 0:cw]),
                     reads=[sbuf], writes=[dst_bufs[k]])
            c0 += cw


def layer_norm_tile(p, z, zb, gB, bB, cbuf, outt, outb, sq, sqb, st, stb):
    S = p.S
    S.op("dve", lambda e: e.reduce_sum(out=st[:, 0:1], in_=z[:], axis=AX.X), reads=[zb], writes=[stb])
    S.op("dve", lambda e: e.tensor_tensor(out=sq[:], in0=z[:], in1=z[:], op=ALU.mult), reads=[zb], writes=[sqb])
    S.op("dve", lambda e: e.reduce_sum(out=st[:, 1:2], in_=sq[:], axis=AX.X), reads=[sqb], writes=[stb])
    S.op("dve", lambda e: e.tensor_scalar(out=st[:, 2:3], in0=st[:, 0:1], scalar1=1.0 / D, scalar2=None, op0=ALU.mult),
         reads=[stb], writes=[stb])
    S.op("dve", lambda e: e.tensor_tensor(out=st[:, 3:4], in0=st[:, 2:3], in1=st[:, 2:3], op=ALU.mult),
         reads=[stb], writes=[stb])
    S.op("dve", lambda e: e.scalar_tensor_tensor(out=st[:, 4:5], in0=st[:, 1:2], scalar=1.0 / D, in1=st[:, 3:4],
                                                 op0=ALU.mult, op1=ALU.subtract), reads=[stb], writes=[stb])
    S.op("dve", lambda e: e.tensor_scalar(out=st[:, 6:7], in0=st[:, 4:5], scalar1=LN_EPS, scalar2=None, op0=ALU.add),
         reads=[stb], writes=[stb])
    S.op("act", lambda e: e.activation(out=st[:, 7:8], in_=st[:, 6:7], func=AF.Sqrt), reads=[stb], writes=[stb])
    S.op("dve", lambda e: e.reciprocal(out=st[:, 5:6], in_=st[:, 7:8]), reads=[stb], writes=[stb])
    S.op("dve", lambda e: e.tensor_scalar(out=sq[:], in0=z[:], scalar1=st[:, 2:3], scalar2=st[:, 5:6],
                                          op0=ALU.subtract, op1=ALU.mult), reads=[zb, stb], writes=[sqb])
    S.op("dve", lambda e: e.tensor_tensor(out=sq[:], in0=sq[:], in1=gB[:], op=ALU.mult), reads=[sqb, cbuf], writes=[sqb])
    S.op("dve", lambda e: e.tensor_tensor(out=outt[:], in0=sq[:], in1=bB[:], op=ALU.add), reads=[sqb, cbuf], writes=[outb])


def layer_norm_multi(p, tiles, gB, bB, cbuf):
    S = p.S

    def steps_for(z, zb, outt, outb, sq, sqb, st, stb):
        return [
            lambda: S.op("dve", lambda e: e.reduce_sum(out=st[:, 0:1], in_=z[:], axis=AX.X), reads=[zb], writes=[stb]),
            lambda: S.op("dve", lambda e: e.tensor_tensor(out=sq[:], in0=z[:], in1=z[:], op=ALU.mult), reads=[zb], writes=[sqb]),
            lambda: S.op("dve", lambda e: e.reduce_sum(out=st[:, 1:2], in_=sq[:], axis=AX.X), reads=[sqb], writes=[stb]),
            lambda: S.op("dve", lambda e: e.tensor_scalar(out=st[:, 2:3], in0=st[:, 0:1], scalar1=1.0 / D, scalar2=None, op0=ALU.mult),
                         reads=[stb], writes=[stb]),
            lambda: S.op("dve", lambda e: e.tensor_tensor(out=st[:, 3:4], in0=st[:, 2:3], in1=st[:, 2:3], op=ALU.mult),
                         reads=[stb], writes=[stb]),
            lambda: S.op("dve", lambda e: e.scalar_tensor_tensor(out=st[:, 4:5], in0=st[:, 1:2], scalar=1.0 / D, in1=st[:, 3:4],
                                                                 op0=ALU.mult, op1=ALU.subtract), reads=[stb], writes=[stb]),
            lambda: S.op("dve", lambda e: e.tensor_scalar(out=st[:, 6:7], in0=st[:, 4:5], scalar1=LN_EPS, scalar2=None, op0=ALU.add),
                         reads=[stb], writes=[stb]),
            lambda: S.op("act", lambda e: e.activation(out=st[:, 7:8], in_=st[:, 6:7], func=AF.Sqrt), reads=[stb], writes=[stb]),
            lambda: S.op("dve", lambda e: e.reciprocal(out=st[:, 5:6], in_=st[:, 7:8]), reads=[stb], writes=[stb]),
            lambda: S.op("dve", lambda e: e.tensor_scalar(out=sq[:], in0=z[:], scalar1=st[:, 2:3], scalar2=st[:, 5:6],
                                                          op0=ALU.subtract, op1=ALU.mult), reads=[zb, stb], writes=[sqb]),
            lambda: S.op("dve", lambda e: e.tensor_tensor(out=sq[:], in0=sq[:], in1=gB[:], op=ALU.mult), reads=[sqb, cbuf], writes=[sqb]),
            lambda: S.op("dve", lambda e: e.tensor_tensor(out=outt[:], in0=sq[:], in1=bB[:], op=ALU.add), reads=[sqb, cbuf], writes=[outb]),
        ]

    lists = [steps_for(*tl) for tl in tiles]
    for i in range(len(lists[0])):
        for lst in lists:
            lst[i]()


def layer_norm_pair_inplace(p, tiles, sq, sqb, gB, bB, cbuf):
    S = p.S
    for (z, zb, st, stb) in tiles:
        S.op("dve", lambda e, z=z, st=st: e.reduce_sum(out=st[:, 0:1], in_=z[:], axis=AX.X), reads=[zb], writes=[stb])
        S.op("dve", lambda e, z=z: e.tensor_tensor(out=sq[:], in0=z[:], in1=z[:], op=ALU.mult), reads=[zb], writes=[sqb])
        S.op("dve", lambda e, st=st: e.reduce_sum(out=st[:, 1:2], in_=sq[:], axis=AX.X), reads=[sqb], writes=[stb])

    def steps_for(z, zb, st, stb):
        return [
            lambda: S.op("dve", lambda e: e.tensor_scalar(out=st[:, 2:3], in0=st[:, 0:1], scalar1=1.0 / D, scalar2=None, op0=ALU.mult),
                         reads=[stb], writes=[stb]),
            lambda: S.op("dve", lambda e: e.tensor_tensor(out=st[:, 3:4], in0=st[:, 2:3], in1=st[:, 2:3], op=ALU.mult),
                         reads=[stb], writes=[stb]),
            lambda: S.op("dve", lambda e: e.scalar_tensor_tensor(out=st[:, 4:5], in0=st[:, 1:2], scalar=1.0 / D, in1=st[:, 3:4],
                                                                 op0=ALU.mult, op1=ALU.subtract), reads=[stb], writes=[stb]),
            lambda: S.op("dve", lambda e: e.tensor_scalar(out=st[:, 6:7], in0=st[:, 4:5], scalar1=LN_EPS, scalar2=None, op0=ALU.add),
                         reads=[stb], writes=[stb]),
            lambda: S.op("act", lambda e: e.activation(out=st[:, 7:8], in_=st[:, 6:7], func=AF.Sqrt), reads=[stb], writes=[stb]),
            lambda: S.op("dve", lambda e: e.reciprocal(out=st[:, 5:6], in_=st[:, 7:8]), reads=[stb], writes=[stb]),
            lambda: S.op("dve", lambda e: e.tensor_scalar(out=z[:], in0=z[:], scalar1=st[:, 2:3], scalar2=st[:, 5:6],
                                                          op0=ALU.subtract, op1=ALU.mult), reads=[zb, stb], writes=[zb]),
            lambda: S.op("dve", lambda e: e.tensor_tensor(out=z[:], in0=z[:], in1=gB[:], op=ALU.mult), reads=[zb, cbuf], writes=[zb]),
            lambda: S.op("dve", lambda e: e.tensor_tensor(out=z[:], in0=z[:], in1=bB[:], op=ALU.add), reads=[zb, cbuf], writes=[zb]),
        ]

    lists = [steps_for(*tl) for tl in tiles]
    for i in range(len(lists[0])):
        for lst in lists:
            lst[i]()


def transpose_tile(p, src, srcb, ident, identb, psT, dstT_view, dstb, nchunk=8, evac="act"):
    S = p.S
    pt, ptb = psT.next()
    for c in range(nchunk):
        S.op("pe", lambda e, c=c, pt=pt: e.transpose(out=pt[:, c, :], in_=src[:, c * 128:(c + 1) * 128], identity=ident[:]),
             reads=[srcb, identb], writes=[ptb])
    if evac == "act":
        S.op("act", lambda e, pt=pt: e.copy(out=dstT_view, in_=pt[:, 0:nchunk, :]), reads=[ptb], writes=[dstb])
    else:
        S.op(evac, lambda e, pt=pt: e.tensor_copy(out=dstT_view, in_=pt[:, 0:nchunk, :]), reads=[ptb], writes=[dstb])


RET_G = [(1.0 - 2.0 ** (-5.0 - h)) ** 128 for h in range(4)]


def phase_l0a(nc, x, w_in, rot, ident_d, mask_d, retg_d, yT_s, qkdT_s, vd_s, ntiles=NT):
    p = P(nc, "l0a")
    S = p.S
    Win = p.sb([128, 8, 1792], BF16, "win")
    winb = bufs("win", 8)
    stage = p.rot_sb(4, [128, 1792], F32, "stage")
    ident = p.sb([128, 128], BF16, "ident")
    identf = p.sb([128, 128], F32, "identf")
    mask4 = p.sb([128, 2, 128], F32, "mask4")
    retg = p.sb([128, 2], F32, "retg")
    cb = Buf("const")
    dma(S, "sp", identf[:], ident_d[:, :], writes=[cb])
    for h in range(2):
        dma(S, "sp", mask4[:, h, :], mask_d[:, :], writes=[cb])
    dma(S, "sp", retg[:], retg_d[:, :], writes=[cb])
    S.op("dve", lambda e: e.tensor_copy(out=ident[:], in_=identf[:]), reads=[cb], writes=[cb])
    load_weight(p, Win, winb, w_in, 8, 1792, stage, cast_engs=("act", "dve"))

    xf = p.rot_sb(2, [128, 1024], F32, "xf")
    xb = p.rot_sb(2, [128, 1024], BF16, "xb")
    xT = p.rot_sb(2, [128, 8, 512], BF16, "xT")
    tab = p.rot_sb(2, [128, 4, 128], F32, "tab")
    psT = p.rot_ps(2, [128, 8, 128], BF16, "psT")
    psF = p.rot_ps(6, [128, 512], F32, "psF")
    tmp = p.rot_sb(4, [128, 2, 64], F32, "tmp")
    Qa = p.rot_sb(3, [128, 256], BF16, "Qa")
    Kb = p.rot_sb(3, [128, 256], BF16, "Kb")
    V = p.rot_sb(3, [128, 256], BF16, "V")
    G = p.rot_sb(3, [128, 256], F32, "G")
    Vd = p.rot_sb(2, [128, 256], BF16, "Vd")
    QKT = p.rot_sb(2, [128, 4, 128], BF16, "QKT")
    PT = p.rot_sb(2, [128, 2, 128], BF16, "PT")
    R32 = p.sb([128, 2, 128], F32, "R32")
    Rbf = p.sb([128, 2, 128], BF16, "Rbf")
    r32b, rbfb = Buf("r32"), Buf("rbf")
    sqt = p.rot_sb(2, [128, 256], F32, "sqt")
    ysr = p.rot_sb(2, [128, 256], F32, "ysr")
    stt = p.rot_sb(2, [128, 32], F32, "stt")
    yc = p.rot_sb(2, [128, 2, 128], F32, "yc")
    yg = p.rot_sb(3, [128, 256], BF16, "yg")
    yTt = p.rot_sb(2, [128, 2, 128], BF16, "yTt")
    qke = p.rot_sb(2, [128, 512], BF16, "qke")
    ytb = bufs("yTs", 1)[0]
    qkb, vdb = Buf("qkd"), Buf("vds")

    ngroups = (ntiles + 3) // 4
    state = {}

    def stage_a(t):
        g, tt = divmod(t, 4)
        xTt, xTb = state[("xT", g)]
        cols = slice(tt * 128, (tt + 1) * 128)
        tb_, tbb = tab.next()
        dma(S, "sp", tb_[:], rot[:, t * 128:(t + 1) * 128, :].rearrange("j p c -> p j c"), writes=[tbb])

        def proj(c0, wd=512):
            ps, psb = psF.next()
            for k in range(8):
                S.op("pe", lambda e, k=k, ps=ps: e.matmul(ps[:, 0:wd], lhsT=xTt[:, k, cols], rhs=Win[:, k, c0:c0 + wd],
                                                          start=(k == 0), stop=(k == 7)),
                     reads=[xTb, winb[k]], writes=[psb])
            return ps, psb

        def rotary(psv, psb, jc, js, dst, dstb):
            v = psv.rearrange("p (h d) -> p h d", h=2)
            x1 = v[:, :, 0:64]
            x2 = v[:, :, 64:128]
            cs = tb_[:, jc, :].rearrange("p (h d) -> p h d", h=2)
            sn = tb_[:, js, :].rearrange("p (h d) -> p h d", h=2)
            dv = dst[:].rearrange("p (h d) -> p h d", h=2)
            t1, t1b = tmp.next()
            t2, t2b = tmp.next()
            S.op("dve", lambda e: e.tensor_tensor(out=t1[:], in0=x1, in1=cs, op=ALU.mult), reads=[psb, tbb], writes=[t1b])
            S.op("dve", lambda e: e.tensor_tensor(out=t2[:], in0=x2, in1=sn, op=ALU.mult), reads=[psb, tbb], writes=[t2b])
            S.op("dve", lambda e: e.tensor_tensor(out=dv[:, :, 0:64], in0=t1[:], in1=t2[:], op=ALU.subtract),
                 reads=[t1b, t2b], writes=[dstb])
            t3, t3b = tmp.next()
            t4, t4b = tmp.next()
            S.op("dve", lambda e: e.tensor_tensor(out=t3[:], in0=x1, in1=sn, op=ALU.mult), reads=[psb, tbb], writes=[t3b])
            S.op("dve", lambda e: e.tensor_tensor(out=t4[:], in0=x2, in1=cs, op=ALU.mult), reads=[psb, tbb], writes=[t4b])
            S.op("dve", lambda e: e.tensor_tensor(out=dv[:, :, 64:128], in0=t3[:], in1=t4[:], op=ALU.add),
                 reads=[t3b, t4b], writes=[dstb])

        qa, qab = Qa.next()
        kb_, kbb = Kb.next()
        vv, vb = V.next()
        gg, gb = G.next()
        ps, psb = proj(0)
        rotary(ps[:, 0:256], psb, 0, 1, qa, qab)
        rotary(ps[:, 256:512], psb, 2, 3, kb_, kbb)
        ps, psb = proj(512)
        S.op("act", lambda e, ps=ps: e.copy(out=vv[:], in_=ps[:, 0:256]), reads=[psb], writes=[vb])
        S.op("act", lambda e, ps=ps: e.activation(out=gg[:], in_=ps[:, 256:512], func=AF.Silu), reads=[psb], writes=[gb])
        ps, psb = proj(1536, 256)
        vd, vdbuf = Vd.next()
        S.op("act", lambda e, ps=ps: e.copy(out=vd[:], in_=ps[:, 0:256]), reads=[psb], writes=[vdbuf])
        dma(S, "pool", vd_s[t * 128:(t + 1) * 128, :], vd[:], reads=[vdbuf], writes=[vdb])
        state[("t", t)] = (qa, qab, kb_, kbb, vv, vb, gg, gb)

    def stage_b(t):
        qa, qab, kb_, kbb, vv, vb, gg, gb = state.pop(("t", t))
        qkt, qktb = QKT.next()
        pt, ptb = psT.next()
        for h in range(2):
            S.op("pe", lambda e, h=h: e.transpose(out=pt[:, h, :], in_=qa[:, h * 128:(h + 1) * 128], identity=ident[:]),
                 reads=[qab, cb], writes=[ptb])
        for h in range(2):
            S.op("pe", lambda e, h=h: e.transpose(out=pt[:, 2 + h, :], in_=kb_[:, h * 128:(h + 1) * 128], identity=ident[:]),
                 reads=[kbb, cb], writes=[ptb])
        S.op("act", lambda e: e.copy(out=qkt[:], in_=pt[:, 0:4, :]), reads=[ptb], writes=[qktb])
        pss, pssb = psF.next()
        for h in range(2):
            S.op("pe", lambda e, h=h: e.matmul(pss[:, h * 128:(h + 1) * 128], lhsT=qkt[:, 2 + h, :], rhs=qkt[:, h, :],
                                               start=True, stop=True), reads=[qktb], writes=[pssb])
        ptt, pttb = PT.next()
        S.op("dve", lambda e: e.tensor_tensor(out=ptt[:].rearrange("p h c -> p (h c)"), in0=pss[:, 0:256],
                                              in1=mask4[:].rearrange("p h c -> p (h c)"), op=ALU.mult),
             reads=[pssb, cb], writes=[pttb])
        BL = int(os.environ.get("B_LEVEL", "9"))
        if BL < 2:
            return
        do_state = t < ntiles - 1
        if do_state:
            pskv, pskvb = psF.next()
            for h in range(2):
                hs = slice(h * 128, (h + 1) * 128)
                S.op("pe", lambda e, hs=hs: e.matmul(pskv[:, hs], lhsT=kb_[:, hs], rhs=vv[:, hs], start=True, stop=True),
                     reads=[kbb, vb], writes=[pskvb])
        psy, psyb = psF.next()
        for h in range(2):
            hs = slice(h * 128, (h + 1) * 128)
            S.op("pe", lambda e, h=h, hs=hs: e.matmul(psy[:, hs], lhsT=ptt[:, h, :], rhs=vv[:, hs], start=True, stop=(t == 0)),
                 reads=[pttb, vb], writes=[psyb])
            if t > 0:
                S.op("pe", lambda e, h=h, hs=hs: e.matmul(psy[:, hs], lhsT=qkt[:, h, :], rhs=Rbf[:, h, :], start=False, stop=True),
                     reads=[qktb, rbfb], writes=[psyb])
        if do_state:
            for h in range(2):
                hs = slice(h * 128, (h + 1) * 128)
                if t == 0:
                    S.op("dve", lambda e, h=h, hs=hs: e.tensor_copy(out=R32[:, h, :], in_=pskv[:, hs]), reads=[pskvb], writes=[r32b])
                else:
                    S.op("dve", lambda e, h=h, hs=hs: e.scalar_tensor_tensor(out=R32[:, h, :], in0=R32[:, h, :], scalar=retg[:, h:h + 1],
                                                                             in1=pskv[:, hs], op0=ALU.mult, op1=ALU.add),
                         reads=[pskvb, r32b, cb], writes=[r32b])
            for h in range(2):
                S.op("act", lambda e, h=h: e.activation(out=Rbf[:, h, :], in_=R32[:, h, :], func=AF.Identity, scale=retg[:, h:h + 1]),
                     reads=[r32b], writes=[rbfb])
        if BL < 4:
            return
        st, stb = stt.next()
        sq, sqb = sqt.next()
        ycc, ycb = yc.next()
        ygg, ygb = yg.next()
        hn_n = [0]
        hn_max = int(os.environ.get("HN_OPS", "99"))

        def HN(*a, **k):
            hn_n[0] += 1
            if hn_n[0] <= hn_max:
                S.op(*a, **k)
        ysb, ysbb = ysr.next()
        HN("act", lambda e: e.copy(out=ysb[:], in_=psy[:, 0:256]), reads=[psyb], writes=[ysbb])
        y4 = ysb[:].rearrange("p (h d) -> p h d", h=2)
        HN("dve", lambda e: e.reduce_sum(out=st[:, 0:2], in_=y4, axis=AX.X), reads=[ysbb], writes=[stb])
        HN("dve", lambda e: e.tensor_tensor(out=sq[:], in0=ysb[:], in1=ysb[:], op=ALU.mult), reads=[ysbb], writes=[sqb])
        HN("dve", lambda e: e.reduce_sum(out=st[:, 4:6], in_=sq[:].rearrange("p (h d) -> p h d", h=2), axis=AX.X),
             reads=[sqb], writes=[stb])
        HN("dve", lambda e: e.tensor_scalar(out=st[:, 8:10], in0=st[:, 0:2], scalar1=1.0 / 128, scalar2=None, op0=ALU.mult),
             reads=[stb], writes=[stb])
        HN("dve", lambda e: e.tensor_tensor(out=st[:, 12:14], in0=st[:, 8:10], in1=st[:, 8:10], op=ALU.mult),
             reads=[stb], writes=[stb])
        HN("dve", lambda e: e.scalar_tensor_tensor(out=st[:, 16:18], in0=st[:, 4:6], scalar=1.0 / 128, in1=st[:, 12:14],
                                                     op0=ALU.mult, op1=ALU.subtract), reads=[stb], writes=[stb])
        HN("dve", lambda e: e.tensor_scalar(out=st[:, 24:26], in0=st[:, 16:18], scalar1=LN_EPS, scalar2=None, op0=ALU.add),
             reads=[stb], writes=[stb])
        HN("act", lambda e: e.activation(out=st[:, 28:30], in_=st[:, 24:26], func=AF.Sqrt), reads=[stb], writes=[stb])
        HN("dve", lambda e: e.reciprocal(out=st[:, 20:22], in_=st[:, 28:30]), reads=[stb], writes=[stb])
        for h in range(2):
            HN("dve", lambda e, h=h: e.tensor_scalar(out=ycc[:, h, :], in0=ysb[:, h * 128:(h + 1) * 128],
                                                       scalar1=st[:, 8 + h:9 + h], scalar2=st[:, 20 + h:21 + h],
                                                       op0=ALU.subtract, op1=ALU.mult), reads=[ysbb, stb], writes=[ycb])
        HN("dve", lambda e: e.tensor_tensor(out=ygg[:], in0=ycc[:].rearrange("p h d -> p (h d)"), in1=gg[:], op=ALU.mult),
             reads=[ycb, gb], writes=[ygb])
        state[("c", t)] = (ygg, ygb)

    def stage_c(t):
        ygg, ygb = state.pop(("c", t))
        yt, ytbuf = yTt.next()
        pt2, pt2b = psT.next()
        for h in range(2):
            S.op("pe", lambda e, h=h: e.transpose(out=pt2[:, h, :], in_=ygg[:, h * 128:(h + 1) * 128], identity=ident[:]),
                 reads=[ygb, cb], writes=[pt2b])
        S.op("act", lambda e: e.copy(out=yt[:], in_=pt2[:, 0:2, :]), reads=[pt2b], writes=[ytbuf])
        dma(S, "pool", yT_s[0:256, t * 128:(t + 1) * 128].rearrange("(c p) t -> p c t", p=128), yt[:], reads=[ytbuf], writes=[ytb])

    def group_front(g):
        xTt, xTb = xT.next()
        state[("xT", g)] = (xTt, xTb)
        for tt in range(4):
            t = g * 4 + tt
            if t >= ntiles:
                break
            xft, xfb = xf.next()
            dma(S, "sp", xft[:], x[t * 128:(t + 1) * 128, :], writes=[xfb])
            xbt, xbb = xb.next()
            S.op("act", lambda e, xbt=xbt, xft=xft: e.copy(out=xbt[:], in_=xft[:]), reads=[xfb], writes=[xbb])
            transpose_tile(p, xbt, xbb, ident, cb, psT, xTt[:, :, tt * 128:(tt + 1) * 128], xTb)

    def group_back(g):
        xTt, xTb = state[("xT", g)]
        ncol = min(512, (ntiles - g * 4) * 128)
        for fc in range(4):
            ps, psb = psF.next()
            for k in range(8):
                S.op("pe", lambda e, k=k, ps=ps, fc=fc: e.matmul(ps[:, 0:ncol], lhsT=Win[:, k, 1024 + fc * 128:1024 + (fc + 1) * 128],
                                                               rhs=xTt[:, k, 0:ncol], start=(k == 0), stop=(k == 7)),
                     reads=[xTb, winb[k]], writes=[psb])
            q, qb_ = qke.next()
            S.op("act", lambda e, ps=ps, q=q: e.copy(out=q[:, 0:ncol], in_=ps[:, 0:ncol]), reads=[psb], writes=[qb_])
            dma(S, "pool", qkdT_s[fc * 128:(fc + 1) * 128, g * 512:g * 512 + ncol], q[:, 0:ncol], reads=[qb_], writes=[qkb])

    order = []
    for g in range(ngroups):
        group_front(g)
        for tt in range(4):
            t = g * 4 + tt
            if t >= ntiles:
                break
            stage_a(t)
            if t >= 1:
                stage_b(t - 1)
            if t >= 2:
                stage_c(t - 2)
        group_back(g)
    stage_b(ntiles - 1)
    if ntiles >= 2:
        stage_c(ntiles - 2)
    stage_c(ntiles - 1)
    p.finish()


DSA_PAT = (1, 4, 16)


def phase_l0b(nc, qkdT_s, vd_s, bm_d, yT_s, heads=range(4)):
    p = P(nc, "l0b")
    S = p.S
    QT = p.rot_sb(2, [64, S_LEN], BF16, "QT")
    KT = p.rot_sb(2, [64, S_LEN], BF16, "KT")
    VA = [p.rot_sb(2, [128, 32, 128], BF16, f"VA{i}") for i in range(3)]
    BM = p.rot_sb(2, [128, 3, 256], F32, "BM")
    acc = p.rot_sb(2, [128, S_LEN], F32, "acc")
    rd = p.rot_sb(1, [64, S_LEN], F32, "rd")
    yd = p.rot_sb(2, [64, S_LEN], BF16, "yd")
    Lg = p.rot_sb(4, [128, 256], F32, "Lg")
    Pe = p.rot_sb(5, [128, 256], BF16, "Pe")
    psS = p.rot_ps(4, [128, 256], F32, "psS")
    psO = p.rot_ps(3, [128, 128], F32, "psO")
    ydb = Buf("yd_s")
    vdt = vd_s.tensor
    vab = {}
    for i, r in enumerate(DSA_PAT):
        for j in range(2):
            vab[(i, j)] = bufs(f"va{i}{j}", r)
            t_ = VA[i].t[j]
            S.op("pool", lambda e, t_=t_: e.memset(t_[:, :, 64:128], 1.0), writes=vab[(i, j)])
    def do_head(h):
        qt, qtb = QT.next()
        kt, ktb = KT.next()
        dma(S, "sp", qt[:], qkdT_s[h * 64:(h + 1) * 64, :], writes=[qtb])
        dma(S, "sp", kt[:], qkdT_s[256 + h * 64:256 + (h + 1) * 64, :], writes=[ktb])
        bm, bmb = BM.next()
        dma(S, "sp", bm[:], bm_d[h], writes=[bmb])
        va = []
        for i, r in enumerate(DSA_PAT):
            slot = VA[i].i % 2
            t_, _unused = VA[i].next()
            bl = vab[(i, slot)]
            nb = 32 // r
            for res in range(r):
                src = bass.AP(vdt, (res * 256) + h * 64, [[r * 256, 128], [128 * r * 256, nb], [1, 64]])
                dma(S, "sp", t_[:, res * nb:(res + 1) * nb, 0:64], src, writes=[bl[res]])
            va.append((t_, bl))
        ac, acb = acc.next()
        blocks = []
        for i, r in enumerate(DSA_PAT):
            nb = 32 // r
            for res in range(r):
                for n in range(nb):
                    blocks.append((i, r, nb, res, n))

        def stage1(i, r, nb, res, n):
            c0 = res + 128 * r * n
            qs = slice(c0, c0 + 127 * r + 1, r)
            ps, psb = psS.next()
            S.op("pe", lambda e: e.matmul(ps[:, 0:128], lhsT=kt[:, qs], rhs=qt[:, qs], start=True, stop=True),
                 reads=[ktb, qtb], writes=[psb])
            w = 128
            if n > 0:
                c1 = res + 128 * r * (n - 1)
                ks = slice(c1, c1 + 127 * r + 1, r)
                S.op("pe", lambda e: e.matmul(ps[:, 128:256], lhsT=kt[:, ks], rhs=qt[:, qs], start=True, stop=True),
                     reads=[ktb, qtb], writes=[psb])
                w = 256
            lg, lgb = Lg.next()
            S.op("dve", lambda e: e.scalar_tensor_tensor(out=lg[:, 0:w], in0=ps[:, 0:w], scalar=0.125,
                                                         in1=bm[:, i, 0:w], op0=ALU.mult, op1=ALU.add),
                 reads=[psb, bmb], writes=[lgb])
            pe_, peb = Pe.next()
            S.op("act", lambda e: e.activation(out=pe_[:, 0:w], in_=lg[:, 0:w], func=AF.Exp),
                 reads=[lgb], writes=[peb])
            return (pe_, peb, qs)

        def stage2(i, r, nb, res, n, pe_, peb, qs):
            vt, vbl = va[i]
            vtb = vbl[res]
            po, pob = psO.next()
            ti = res * nb + n
            S.op("pe", lambda e: e.matmul(po[:], lhsT=vt[:, ti, :], rhs=pe_[:, 0:128], start=True, stop=(n == 0)),
                 reads=[vtb, peb], writes=[pob])
            if n > 0:
                S.op("pe", lambda e: e.matmul(po[:], lhsT=vt[:, ti - 1, :], rhs=pe_[:, 128:256], start=False, stop=True),
                     reads=[vtb, peb], writes=[pob])
            if i == 0:
                S.op("act", lambda e: e.copy(out=ac[:, qs], in_=po[:]), reads=[pob], writes=[acb])
            else:
                S.op("dve", lambda e: e.tensor_tensor(out=ac[:, qs], in0=ac[:, qs], in1=po[:], op=ALU.add),
                     reads=[pob, acb], writes=[acb])

        LAG = 2
        pend = []
        for bi, blk in enumerate(blocks):
            pend.append(stage1(*blk))
            if bi >= LAG:
                stage2(*blocks[bi - LAG], *pend[bi - LAG])
        for bi in range(max(0, len(blocks) - LAG), len(blocks)):
            stage2(*blocks[bi], *pend[bi])
        rdt, rdb = rd.next()
        ydt, ydtb = yd.next()
        S.op("dve", lambda e, ac=ac, rdt=rdt: e.reciprocal(out=rdt[:], in_=ac[64:128, :]), reads=[acb], writes=[rdb])
        S.op("dve", lambda e, ac=ac, rdt=rdt, ydt=ydt: e.tensor_tensor(out=ydt[:], in0=ac[0:64, :], in1=rdt[:], op=ALU.mult),
             reads=[acb, rdb], writes=[ydtb])
        dma(S, "pool", yT_s[256 + h * 64:256 + (h + 1) * 64, :], ydt[:], reads=[ydtb], writes=[ydb])

    for h in heads:
        do_head(h)
    p.finish()


def bcast_rows(ap_row, n=128):
    return bass.AP(ap_row.tensor, ap_row.offset, [[0, n]] + [list(a) for a in ap_row.ap])


def emit_ln_and_store(p, z, zb, gB, bB, cbuf, xo_rot, xob_rot, sqr, stt, ident, psT, xoT, xoTb, tt, x_out, t):
    S = p.S
    xo, xob = xo_rot.next() if xo_rot is not None else (z, zb)
    sq, sqb = sqr.next()
    st, stb = stt.next()
    layer_norm_tile(p, z, zb, gB, bB, cbuf, xo, xob, sq, sqb, st, stb)
    dma(S, "pool", x_out[t * 128:(t + 1) * 128, :], xo[:], reads=[xob], writes=[Buf()])
    if xoT is not None:
        xbt, xbb = xob_rot.next()
        S.op("act", lambda e: e.copy(out=xbt[:], in_=xo[:]), reads=[xob], writes=[xbb])
        transpose_tile(p, xbt, xbb, ident, cbuf, psT, xoT[:, :, tt * 128:(tt + 1) * 128], xoTb, evac="act")


def phase_outproj(nc, name, x_in, yT_s, sel_d, w_out, ln_g, ln_b, ident_d, x_out, xT_out, ntiles=16, halo_x=None, xhT_out=None):
    p = P(nc, name)
    S = p.S
    Wout = p.sb([128, 8, 1024], BF16, "wout")
    wb = bufs("w", 8)
    stage = p.rot_sb(4, [128, 1024], F32, "stage")
    ident = p.sb([128, 128], BF16, "ident")
    identf = p.sb([128, 128], F32, "identf")
    gB = p.sb([128, 1024], F32, "gB")
    bB = p.sb([128, 1024], F32, "bB")
    cb = Buf("const")
    sel = p.sb([128, 2], F32, "sel")
    dma(S, "sp", sel[:], sel_d[:, :], writes=[cb])
    dma(S, "sp", identf[:], ident_d[:, :], writes=[cb])
    dma(S, "sp", gB[:], bcast_rows(ln_g), writes=[cb])
    dma(S, "sp", bB[:], bcast_rows(ln_b), writes=[cb])
    S.op("dve", lambda e: e.tensor_copy(out=ident[:], in_=identf[:]), reads=[cb], writes=[cb])
    load_weight(p, Wout, wb, w_out, 8, 1024, stage, cast_engs=("act", "dve"), stage_cols=1024)
    yTg = p.rot_sb(2, [128, 8, 512], BF16, "yTg")
    yTa = p.rot_sb(2, [128, 8, 512], BF16, "yTa")
    yTb = p.rot_sb(2, [128, 8, 512], BF16, "yTb")
    xf = p.rot_sb(3, [128, 1024], F32, "xf")
    zt = p.rot_sb(5, [128, 1024], F32, "z")
    xo = p.rot_sb(4, [128, 1024], F32, "xo")
    xob = p.rot_sb(4, [128, 1024], BF16, "xob")
    sqr = p.rot_sb(2, [128, 1024], F32, "sq")
    stt = p.rot_sb(2, [128, 8], F32, "st")
    xoT = p.rot_sb(2, [128, 8, 512], BF16, "xoT")
    psF = p.rot_ps(4, [128, 512], F32, "psF")
    psT = p.rot_ps(2, [128, 8, 128], BF16, "psT")
    ngroups = (ntiles + 3) // 4
    pending = []
    cur_pair = []

    def flush_pair():
        if not pending:
            return
        tl = []
        for (z, zb, xoTt, xoTb, tt, t, g, ncol, last) in pending:
            xo_, xob_ = xo.next()
            sq_, sqb_ = sqr.next()
            st_, stb_ = stt.next()
            tl.append((z, zb, xo_, xob_, sq_, sqb_, st_, stb_))
        layer_norm_multi(p, tl, gB, bB, cb)
        for (z, zb, xoTt, xoTb, tt, t, g, ncol, last), (_, _, xo_, xob_, _, _, _, _) in zip(pending, tl):
            dma(S, "pool", x_out[t * 128:(t + 1) * 128, :], xo_[:], reads=[xob_], writes=[Buf()])
            xbt, xbb = xob.next()
            S.op("act", lambda e, xbt=xbt, xo_=xo_: e.copy(out=xbt[:], in_=xo_[:]), reads=[xob_], writes=[xbb])
            transpose_tile(p, xbt, xbb, ident, cb, psT, xoTt[:, :, tt * 128:(tt + 1) * 128], xoTb, evac="act")
            if last:
                dma(S, "pool", xT_out[:, g * 512:g * 512 + ncol].rearrange("(c p) t -> p c t", p=128), xoTt[:, :, 0:ncol],
                    reads=[xoTb], writes=[Buf()])
        pending.clear()

    xhT_t = p.sb([128, 8, 128], BF16, "xhT") if halo_x is not None else None

    def do_halo():
        HC = S_LEN // 2 - 128
        yh, yhb = yTa.next()
        dma(S, "sp", yh[:, :, 0:128], yT_s[:, HC:HC + 128].rearrange("(c p) t -> p c t", p=128), writes=[yhb])
        xft, xfb = xf.next()
        dma(S, "sp", xft[:], halo_x, writes=[xfb])
        z, zb = zt.next()
        for nb in range(2):
            ps, psb = psF.next()
            for c in range(8):
                S.op("pe", lambda e, ps=ps, c=c, nb=nb: e.matmul(ps[:], lhsT=yh[:, c, 0:128], rhs=Wout[:, c, nb * 512:(nb + 1) * 512],
                                                              start=(c == 0), stop=(c == 7)), reads=[yhb, wb[c]], writes=[psb])
            S.op("dve", lambda e, ps=ps, nb=nb: e.scalar_tensor_tensor(out=z[:, nb * 512:(nb + 1) * 512], in0=xft[:, nb * 512:(nb + 1) * 512],
                                                                      scalar=ALPHA, in1=ps[:], op0=ALU.mult, op1=ALU.add),
                 reads=[psb, xfb], writes=[zb])
        xo_, xob_ = xo.next()
        sq_, sqb_ = sqr.next()
        st_, stb_ = stt.next()
        layer_norm_tile(p, z, zb, gB, bB, cb, xo_, xob_, sq_, sqb_, st_, stb_)
        xbt, xbb = xob.next()
        S.op("act", lambda e: e.copy(out=xbt[:], in_=xo_[:]), reads=[xob_], writes=[xbb])
        xhT, xhTb = xhT_t, Buf("xhT")
        transpose_tile(p, xbt, xbb, ident, cb, psT, xhT[:, :, 0:128], xhTb, evac="dve")
        dma(S, "pool", xhT_out[:, :].rearrange("(c p) t -> p c t", p=128), xhT[:, :, 0:128], reads=[xhTb], writes=[Buf()])

    for g in range(ngroups):
        ncol = min(512, (ntiles - g * 4) * 128)
        if g == 2 and halo_x is not None:
            do_halo()
        yt, ytb = yTg.next()
        ya, yab = yTa.next()
        yb_, ybb = yTb.next()
        HALF = S_LEN // 2
        dma(S, "sp", ya[:], yT_s[:, g * 512:(g + 1) * 512].rearrange("(c p) t -> p c t", p=128), writes=[yab])
        dma(S, "sp", yb_[:], yT_s[:, HALF + g * 512:HALF + (g + 1) * 512].rearrange("(c p) t -> p c t", p=128), writes=[ybb])
        S.op("act", lambda e, yb_=yb_: e.activation(out=yb_[:], in_=yb_[:], func=AF.Identity, scale=sel[:, 1:2]),
             reads=[ybb, cb], writes=[ybb])
        S.op("dve", lambda e, ya=ya, yb_=yb_, yt=yt: e.scalar_tensor_tensor(out=yt[:], in0=ya[:], scalar=sel[:, 0:1], in1=yb_[:], op0=ALU.mult, op1=ALU.add),
             reads=[yab, ybb, cb], writes=[ytb])
        xoTt, xoTb = xoT.next()
        for tt in range(ncol // 128):
            t = g * 4 + tt
            xft, xfb = xf.next()
            dma(S, "sp", xft[:], x_in[t * 128:(t + 1) * 128, :], writes=[xfb])
            z, zb = zt.next()
            for nb in range(2):
                ps, psb = psF.next()
                for c in range(8):
                    S.op("pe", lambda e, ps=ps, c=c, nb=nb, tt=tt, yt=yt: e.matmul(ps[:], lhsT=yt[:, c, tt * 128:(tt + 1) * 128],
                                                                                  rhs=Wout[:, c, nb * 512:(nb + 1) * 512],
                                                                                  start=(c == 0), stop=(c == 7)),
                         reads=[ytb, wb[c]], writes=[psb])
                S.op("dve", lambda e, ps=ps, nb=nb, z=z, xft=xft: e.scalar_tensor_tensor(out=z[:, nb * 512:(nb + 1) * 512],
                                                                                      in0=xft[:, nb * 512:(nb + 1) * 512], scalar=ALPHA,
                                                                                      in1=ps[:], op0=ALU.mult, op1=ALU.add),
                     reads=[psb, xfb], writes=[zb])
            cur_pair.append((z, zb, xoTt, xoTb, tt, t, g, ncol, tt == ncol // 128 - 1))
            if len(cur_pair) == 2:
                flush_pair()
                pending.extend(cur_pair)
                cur_pair.clear()
    flush_pair()
    pending.extend(cur_pair)
    cur_pair.clear()
    flush_pair()
    p.finish()


def phase_ffn(nc, name, x_in, xT_in, w_up, cw_d, cb_d, w_dn, ln_g, ln_b, ident_d, x_out, xT_out, ntiles=NT, xhT_in=None, sel_d=None):
    p = P(nc, name)
    S = p.S
    Wup = p.sb([128, 8, 2 * D_FF], BF16, "wup")
    wub = bufs("wu", 8)
    Wdn = p.sb([128, NFC, 1024], BF16, "wdn")
    wdb = bufs("wd", NFC)
    stage = p.rot_sb(4, [128, 352], F32, "stage")
    ident = p.sb([128, 128], BF16, "ident")
    identf = p.sb([128, 128], F32, "identf")
    gB = p.sb([128, 1024], F32, "gB")
    bB = p.sb([128, 1024], F32, "bB")
    cw = p.sb([128, NFC, 3], F32, "cw")
    cbias = p.sb([128, NFC], F32, "cbias")
    hl = p.sb([128, NFC, 2], F32, "halo")
    hlb = bufs("hl", NFC)
    cb = Buf("const")
    dma(S, "sp", identf[:], ident_d[:, :], writes=[cb])
    dma(S, "sp", gB[:], bcast_rows(ln_g), writes=[cb])
    dma(S, "sp", bB[:], bcast_rows(ln_b), writes=[cb])
    dma(S, "sp", cw[:], cw_d[:, :, :], writes=[cb])
    dma(S, "sp", cbias[:], cb_d[:, :], writes=[cb])
    S.op("dve", lambda e: e.tensor_copy(out=ident[:], in_=identf[:]), reads=[cb], writes=[cb])
    S.op("pool", lambda e: e.memset(hl[:], 0.0), writes=hlb)
    xTg = p.rot_sb(1, [128, 8, 512], BF16, "xTg")
    xt0, xtb0 = xTg.next()
    nc0 = min(512, ntiles * 128)
    dma(S, "sp", xt0[:, :, 0:nc0], xT_in[:, 0:nc0].rearrange("(c p) t -> p c t", p=128), writes=[xtb0])
    if xhT_in is not None:
        xh = p.sb([128, 8, 2], BF16, "xh")
        selt = p.sb([128, 2], F32, "sel")
        xhb = Buf("xh")
        dma(S, "sp", xh[:], xhT_in[:, 126:128].rearrange("(c p) t -> p c t", p=128), writes=[xhb])
        dma(S, "sp", selt[:], sel_d[:, :], writes=[xhb])
    NBLK = 2
    FPB = NFC // NBLK
    wubb = [bufs(f"wu{b_}", 8) for b_ in range(NBLK)]
    ci = 0
    for blk in range(NBLK):
        for half in range(2):
            for k in range(8):
                for cc in range(0, FPB * 128, 352):
                    c0 = half * D_FF + blk * FPB * 128 + cc
                    st_, stb_ = stage.next()
                    dma(S, "sp", st_[:, 0:352], w_up[k * 128:(k + 1) * 128, c0:c0 + 352], writes=[stb_])
                    if ci % 2 == 0:
                        S.op("act", lambda e, st_=st_, k=k, c0=c0: e.copy(out=Wup[:, k, c0:c0 + 352], in_=st_[:, 0:352]), reads=[stb_], writes=[wubb[blk][k]])
                    else:
                        S.op("dve", lambda e, st_=st_, k=k, c0=c0: e.tensor_copy(out=Wup[:, k, c0:c0 + 352], in_=st_[:, 0:352]), reads=[stb_], writes=[wubb[blk][k]])
                    ci += 1
    gs = p.rot_sb(2, [128, 514], F32, "gs")
    t1r = p.rot_sb(1, [128, 512], F32, "t1")
    hT = p.sb([128, NFC, 512], BF16, "hT")
    hTb = bufs("hT", NFC)
    xf = p.rot_sb(1, [128, 1024], F32, "xf")
    zt = p.rot_sb(2, [128, 1024], F32, "z")
    xo = None
    xob = p.rot_sb(2, [128, 1024], BF16, "xob")
    sqr = p.rot_sb(1, [128, 1024], F32, "sq")
    stt = p.rot_sb(2, [128, 8], F32, "st")
    xoT = p.rot_sb(2, [128, 8, 128], BF16, "xoT")
    psF = p.rot_ps(6, [128, 512], F32, "psF")
    psT = p.rot_ps(2, [128, 8, 128], BF16, "psT")
    ngroups = (ntiles + 3) // 4
    assert xT_out is not None
    pendT = []

    def ln_pair(pr):
        sq, sqb = sqr.next()
        tl = []
        for (z, zb, t) in pr:
            st, stb = stt.next()
            tl.append((z, zb, st, stb))
        layer_norm_pair_inplace(p, tl, sq, sqb, gB, bB, cb)
        for (z, zb, t) in pr:
            dma(S, "pool", x_out[t * 128:(t + 1) * 128, :], z[:], reads=[zb], writes=[Buf()])
            xbt, xbb = xob.next()
            S.op("act", lambda e, xbt=xbt, z=z: e.copy(out=xbt[:], in_=z[:]), reads=[zb], writes=[xbb])
            pendT.append((xbt, xbb, t))

    def flush_T():
        for (xbt, xbb, t) in pendT:
            xoTt, xoTb = xoT.next()
            transpose_tile(p, xbt, xbb, ident, cb, psT, xoTt[:, :, :], xoTb, evac="act")
            dma(S, "pool", xT_out[:, t * 128:(t + 1) * 128].rearrange("(c p) t -> p c t", p=128), xoTt[:, :, :], reads=[xoTb], writes=[Buf()])
        pendT.clear()

    for g in range(ngroups):
        ncol = min(512, (ntiles - g * 4) * 128)
        if g == 0:
            xt, xtb = xt0, xtb0
        else:
            xt, xtb = xTg.next()
            dma(S, "sp", xt[:, :, 0:ncol], xT_in[:, g * 512:g * 512 + ncol].rearrange("(c p) t -> p c t", p=128), writes=[xtb])
        for fc in range(NFC):
            if g == 0 and xhT_in is not None:
                psh, pshb = psF.next()
                for k in range(8):
                    S.op("pe", lambda e, psh=psh, k=k, fc=fc: e.matmul(psh[:, 0:2], lhsT=Wup[:, k, fc * 128:(fc + 1) * 128], rhs=xh[:, k, :],
                                                                      start=(k == 0), stop=(k == 7)), reads=[xhb, wubb[fc // FPB][k]], writes=[pshb])
                S.op("dve", lambda e, psh=psh, fc=fc: e.tensor_scalar(out=hl[:, fc, :], in0=psh[:, 0:2], scalar1=selt[:, 1:2], scalar2=None, op0=ALU.mult),
                     reads=[pshb, xhb, hlb[fc]], writes=[hlb[fc]])
            psg, psgb = psF.next()
            for k in range(8):
                S.op("pe", lambda e, psg=psg, k=k, fc=fc, xt=xt: e.matmul(psg[:, 0:ncol], lhsT=Wup[:, k, fc * 128:(fc + 1) * 128],
                                                                          rhs=xt[:, k, 0:ncol], start=(k == 0), stop=(k == 7)),
                     reads=[xtb, wubb[fc // FPB][k]], writes=[psgb])
            psu, psub = psF.next()
            for k in range(8):
                S.op("pe", lambda e, psu=psu, k=k, fc=fc, xt=xt: e.matmul(psu[:, 0:ncol], lhsT=Wup[:, k, D_FF + fc * 128:D_FF + (fc + 1) * 128],
                                                                          rhs=xt[:, k, 0:ncol], start=(k == 0), stop=(k == 7)),
                     reads=[xtb, wubb[fc // FPB][k]], writes=[psub])
            gt, gtb = gs.next()
            S.op("act", lambda e, gt=gt, fc=fc: e.copy(out=gt[:, 0:2], in_=hl[:, fc, :]), reads=[hlb[fc]], writes=[gtb])
            S.op("act", lambda e, gt=gt, psg=psg: e.copy(out=gt[:, 2:2 + ncol], in_=psg[:, 0:ncol]), reads=[psgb], writes=[gtb])
            S.op("act", lambda e, gt=gt, fc=fc: e.copy(out=hl[:, fc, :], in_=gt[:, ncol:ncol + 2]), reads=[gtb], writes=[hlb[fc]])
            t1, t1b = t1r.next()
            S.op("dve", lambda e, gt=gt, t1=t1, fc=fc: e.tensor_scalar(out=t1[:, 0:ncol], in0=gt[:, 2:2 + ncol], scalar1=cw[:, fc, 2:3],
                                                                     scalar2=None, op0=ALU.mult), reads=[gtb, cb], writes=[t1b])
            S.op("dve", lambda e, gt=gt, t1=t1, fc=fc: e.scalar_tensor_tensor(out=t1[:, 0:ncol], in0=gt[:, 1:1 + ncol], scalar=cw[:, fc, 1:2],
                                                                            in1=t1[:, 0:ncol], op0=ALU.mult, op1=ALU.add),
                 reads=[gtb, cb, t1b], writes=[t1b])
            S.op("dve", lambda e, gt=gt, t1=t1, fc=fc: e.scalar_tensor_tensor(out=t1[:, 0:ncol], in0=gt[:, 0:ncol], scalar=cw[:, fc, 0:1],
                                                                            in1=t1[:, 0:ncol], op0=ALU.mult, op1=ALU.add),
                 reads=[gtb, cb, t1b], writes=[t1b])
            a, ab = t1, t1b
            S.op("act", lambda e, a=a, t1=t1, fc=fc: e.activation(out=a[:, 0:ncol], in_=t1[:, 0:ncol], func=AF.Silu, bias=cbias[:, fc:fc + 1]),
                 reads=[t1b, cb], writes=[ab])
            S.op("dve", lambda e, a=a, psu=psu, fc=fc: e.tensor_tensor(out=hT[:, fc, 0:ncol], in0=a[:, 0:ncol], in1=psu[:, 0:ncol], op=ALU.mult),
                 reads=[ab, psub], writes=[hTb[fc]])
            if g == 0:
                for c0 in range(0, 1024, 352):
                    cw_ = min(352, 1024 - c0)
                    st_, stb_ = stage.next()
                    dma(S, "sp", st_[:, 0:cw_], w_dn[fc * 128:(fc + 1) * 128, c0:c0 + cw_], writes=[stb_])
                    if (fc + c0 // 352) % 2 == 0:
                        S.op("act", lambda e, st_=st_, fc=fc, c0=c0, cw_=cw_: e.copy(out=Wdn[:, fc, c0:c0 + cw_], in_=st_[:, 0:cw_]),
                             reads=[stb_], writes=[wdb[fc]])
                    else:
                        S.op("dve", lambda e, st_=st_, fc=fc, c0=c0, cw_=cw_: e.tensor_copy(out=Wdn[:, fc, c0:c0 + cw_], in_=st_[:, 0:cw_]),
                             reads=[stb_], writes=[wdb[fc]])
        flush_T()
        pair = []
        for tt in range(ncol // 128):
            t = g * 4 + tt
            xft, xfb = xf.next()
            dma(S, "sp", xft[:], x_in[t * 128:(t + 1) * 128, :], writes=[xfb])
            z, zb = zt.next()
            for nb in range(2):
                ps, psb = psF.next()
                for fc in range(NFC):
                    S.op("pe", lambda e, ps=ps, fc=fc, nb=nb, tt=tt: e.matmul(ps[:], lhsT=hT[:, fc, tt * 128:(tt + 1) * 128],
                                                                             rhs=Wdn[:, fc, nb * 512:(nb + 1) * 512],
                                                                             start=(fc == 0), stop=(fc == NFC - 1)),
                         reads=[hTb[fc], wdb[fc]], writes=[psb])
                S.op("dve", lambda e, ps=ps, nb=nb, z=z, xft=xft: e.scalar_tensor_tensor(out=z[:, nb * 512:(nb + 1) * 512],
                                                                                      in0=xft[:, nb * 512:(nb + 1) * 512], scalar=ALPHA,
                                                                                      in1=ps[:], op0=ALU.mult, op1=ALU.add),
                     reads=[psb, xfb], writes=[zb])
            pair.append((z, zb, t))
            if len(pair) == 2:
                if tt == 3:
                    flush_T()
                ln_pair(pair)
                pair = []
        if pair:
            ln_pair(pair)
    flush_T()
    p.finish()


def phase_l1a(nc, xT_in, w_in, cw_d, qkT_s, v_s, o_s, gi_s, gf_s):
    p = P(nc, "l1a")
    S = p.S
    Win = p.sb([128, 8, 2052], BF16, "win")
    wb = bufs("w", 8)
    stage = p.rot_sb(4, [128, 2052], F32, "stage")
    cw = p.sb([128, 8, 4], F32, "cw")
    hl = p.sb([128, 8, 3], F32, "halo")
    hlb = bufs("hl", 8)
    cb = Buf("const")
    dma(S, "sp", cw[:], cw_d[:, :, :], writes=[cb])
    S.op("pool", lambda e: e.memset(hl[:], 0.0), writes=hlb)
    load_weight(p, Win, wb, w_in, 8, 2052, stage, cast_engs=("act", "dve"), stage_cols=2052)
    xTg = p.rot_sb(2, [128, 8, 512], BF16, "xTg")
    gs = p.rot_sb(2, [128, 515], F32, "gs")
    t1r = p.rot_sb(2, [128, 512], F32, "t1")
    qke = p.rot_sb(2, [128, 512], BF16, "qke")
    vt = p.rot_sb(2, [128, 512], BF16, "vt")
    ot = p.rot_sb(2, [128, 512], F32, "ot")
    gr = p.rot_sb(2, [2, 512], F32, "gr")
    psF = p.rot_ps(7, [128, 512], F32, "psF")
    for g in range(NG):
        xt, xtb = xTg.next()
        for j in range(4):
            r0 = j * 512 + (g // 4) * 256
            dma(S, "sp", xt[:, 2 * j:2 * j + 2, :], xT_in[r0:r0 + 256, (g % 4) * 512:(g % 4 + 1) * 512].rearrange("(c p) t -> p c t", p=128), writes=[xtb])
        for fc in range(8):
            ps, psb = psF.next()
            for k in range(8):
                S.op("pe", lambda e, ps=ps, k=k, fc=fc, xt=xt: e.matmul(ps[:], lhsT=Win[:, k, fc * 128:(fc + 1) * 128], rhs=xt[:, k, :],
                                                                        start=(k == 0), stop=(k == 7)), reads=[xtb, wb[k]], writes=[psb])
            gt, gtb = gs.next()
            S.op("act", lambda e, gt=gt, fc=fc: e.copy(out=gt[:, 0:3], in_=hl[:, fc, :]), reads=[hlb[fc]], writes=[gtb])
            S.op("act", lambda e, gt=gt, ps=ps: e.copy(out=gt[:, 3:515], in_=ps[:]), reads=[psb], writes=[gtb])
            S.op("act", lambda e, gt=gt, fc=fc: e.copy(out=hl[:, fc, :], in_=gt[:, 512:515]), reads=[gtb], writes=[hlb[fc]])
            t1, t1b = t1r.next()
            S.op("dve", lambda e, gt=gt, t1=t1, fc=fc: e.tensor_scalar(out=t1[:], in0=gt[:, 3:515], scalar1=cw[:, fc, 3:4], scalar2=None, op0=ALU.mult),
                 reads=[gtb, cb], writes=[t1b])
            for j in (2, 1, 0):
                S.op("dve", lambda e, gt=gt, t1=t1, fc=fc, j=j: e.scalar_tensor_tensor(out=t1[:], in0=gt[:, j:j + 512], scalar=cw[:, fc, j:j + 1],
                                                                                     in1=t1[:], op0=ALU.mult, op1=ALU.add),
                     reads=[gtb, cb, t1b], writes=[t1b])
            q, qb_ = qke.next()
            S.op("act", lambda e, q=q, t1=t1: e.activation(out=q[:], in_=t1[:], func=AF.Silu), reads=[t1b], writes=[qb_])
            dma(S, "pool", qkT_s[fc * 128:(fc + 1) * 128, g * 512:(g + 1) * 512], q[:], reads=[qb_], writes=[Buf()])
        for (c0, dst) in ((2048, gi_s), (2050, gf_s)):
            ps, psb = psF.next()
            for k in range(8):
                S.op("pe", lambda e, ps=ps, k=k, c0=c0, xt=xt: e.matmul(ps[0:2, :], lhsT=Win[:, k, c0:c0 + 2], rhs=xt[:, k, :],
                                                                        start=(k == 0), stop=(k == 7)), reads=[xtb, wb[k]], writes=[psb])
            r_, rb = gr.next()
            S.op("act", lambda e, ps=ps, r_=r_: e.copy(out=r_[:], in_=ps[0:2, :]), reads=[psb], writes=[rb])
            dma(S, "pool", dst[:, g * 512:(g + 1) * 512], r_[:], reads=[rb], writes=[Buf()])
        for tt in range(4):
            t = g * 4 + tt
            v, vb = vt.next()
            o, ob = ot.next()
            ps, psb = psF.next()
            for k in range(8):
                S.op("pe", lambda e, ps=ps, k=k, tt=tt, xt=xt: e.matmul(ps[:], lhsT=xt[:, k, tt * 128:(tt + 1) * 128], rhs=Win[:, k, 1024:1536],
                                                                       start=(k == 0), stop=(k == 7)), reads=[xtb, wb[k]], writes=[psb])
            S.op("act", lambda e, ps=ps, v=v: e.copy(out=v[:], in_=ps[:]), reads=[psb], writes=[vb])
            ps, psb = psF.next()
            for k in range(8):
                S.op("pe", lambda e, ps=ps, k=k, tt=tt, xt=xt: e.matmul(ps[:], lhsT=xt[:, k, tt * 128:(tt + 1) * 128], rhs=Win[:, k, 1536:2048],
                                                                       start=(k == 0), stop=(k == 7)), reads=[xtb, wb[k]], writes=[psb])
            S.op("act", lambda e, ps=ps, o=o: e.activation(out=o[:], in_=ps[:], func=AF.Sigmoid), reads=[psb], writes=[ob])
            dma(S, "pool", v_s[t * 128:(t + 1) * 128, :], v[:], reads=[vb], writes=[Buf()])
            dma(S, "pool", o_s[t * 128:(t + 1) * 128, :], o[:], reads=[ob], writes=[Buf()])
    p.finish()


def phase_l1p(nc, gi_s, gf_s, gb_d, ident_d, e127_d, cols_s):
    p = P(nc, "l1p")
    S = p.S
    N = S_LEN
    gi = p.sb([2, N], F32, "gi")
    gf = p.sb([2, N], F32, "gf")
    A = p.sb([2, N], F32, "A")
    Bt = p.sb([2, N], F32, "B")
    U = p.sb([2, N], F32, "U")
    Mx = p.sb([2, N], F32, "Mx")
    Mg = p.sb([2, 32], F32, "Mg")
    R = [p.sb([2, N], F32, f"R{i}") for i in range(3)]
    gb = p.sb([2, 2], F32, "gb")
    identf = p.sb([128, 128], F32, "identf")
    e127 = p.sb([128, 128], F32, "e127")
    colsb = p.sb([128, 4, 64], F32, "colsb")
    ps3 = [p.ps([128, 32, 2], F32, f"pc{i}") for i in range(3)]
    psg = p.ps([128, 64], F32, "pg")
    b = Buf("all")

    def op(eng, fn):
        S.op(eng, fn, reads=[b], writes=[b])

    dma(S, "sp", gi[:], gi_s[:, :], writes=[b])
    dma(S, "sp", gf[:], gf_s[:, :], writes=[b])
    dma(S, "sp", gb[:], gb_d[:, :], writes=[b])
    dma(S, "sp", identf[:], ident_d[:, :], writes=[b])
    dma(S, "sp", e127[:], e127_d[:, :], writes=[b])
    op("dve", lambda e: e.tensor_scalar(out=gf[:], in0=gf[:], scalar1=gb[:, 1:2], scalar2=None, op0=ALU.add))
    op("act", lambda e: e.activation(out=gf[:], in_=gf[:], func=AF.Exp, scale=-1.0))
    op("dve", lambda e: e.tensor_scalar(out=gf[:], in0=gf[:], scalar1=1.0, scalar2=None, op0=ALU.add))
    op("act", lambda e: e.activation(out=A[:], in_=gf[:], func=AF.Ln))

    def scan(src, tmp, alu):
        cur, oth = src, tmp
        sft = 1
        while sft < N:
            op("act", lambda e, cur=cur, oth=oth, sft=sft: e.copy(out=oth[:, 0:sft], in_=cur[:, 0:sft]))
            op("dve", lambda e, cur=cur, oth=oth, sft=sft: e.tensor_tensor(out=oth[:, sft:N], in0=cur[:, sft:N], in1=cur[:, 0:N - sft], op=alu))
            cur, oth = oth, cur
            sft *= 2
        return cur

    cs = scan(A, Bt, ALU.add)
    op("dve", lambda e: e.tensor_scalar(out=gi[:], in0=gi[:], scalar1=gb[:, 0:1], scalar2=None, op0=ALU.add))
    op("dve", lambda e: e.tensor_tensor(out=U[:], in0=gi[:], in1=cs[:], op=ALU.add))
    op("act", lambda e: e.copy(out=Mx[:], in_=U[:]))
    gcm = scan(Mx, Bt, ALU.max)
    op("dve", lambda e: e.tensor_scalar(out=gcm[:], in0=gcm[:], scalar1=0.0, scalar2=None, op0=ALU.max))
    Mx3 = gcm[:].rearrange("p (n c) -> p n c", c=128)
    op("pool", lambda e: e.memset(Mg[:], 0.0))
    op("dve", lambda e: e.tensor_copy(out=Mg[:, 1:32], in_=Mx3[:, 0:31, 127]))
    Mgb = Mg[:].unsqueeze(2).to_broadcast([2, 32, 128])
    op("dve", lambda e: e.tensor_tensor(out=R[0][:].rearrange("p (n c) -> p n c", c=128), in0=U[:].rearrange("p (n c) -> p n c", c=128),
                                        in1=Mgb, op=ALU.subtract))
    op("dve", lambda e: e.tensor_scalar(out=R[0][:], in0=R[0][:], scalar1=-math.log(16.0), scalar2=None, op0=ALU.add))
    op("act", lambda e: e.activation(out=R[0][:], in_=R[0][:], func=AF.Exp))
    op("dve", lambda e: e.tensor_tensor(out=R[1][:].rearrange("p (n c) -> p n c", c=128), in0=Mx3, in1=Mgb, op=ALU.subtract))
    op("act", lambda e: e.activation(out=R[1][:], in_=R[1][:], func=AF.Exp, scale=-1.0))
    op("dve", lambda e: e.tensor_tensor(out=R[2][:], in0=cs[:], in1=gcm[:], op=ALU.subtract))
    op("act", lambda e: e.activation(out=R[2][:], in_=R[2][:], func=AF.Exp))
    for i in range(3):
        for n in range(32):
            op("pe", lambda e, i=i, n=n: e.transpose(out=ps3[i][:, n, :], in_=R[i][:, n * 128:(n + 1) * 128], identity=identf[0:2, 0:2]))
        op("act", lambda e, i=i: e.copy(out=colsb[:, i, :], in_=ps3[i][:].rearrange("p n h -> p (n h)")))
    op("pe", lambda e: e.matmul(psg[:], lhsT=e127[:], rhs=colsb[:, 1, :], start=True, stop=True))
    op("act", lambda e: e.copy(out=colsb[:, 3, :], in_=psg[:]))
    for i in range(4):
        dma(S, "pool", cols_s[i], colsb[:, i, :], reads=[b], writes=[Buf()])
    p.finish()


def phase_l1b(nc, qkT_s, v_s, o_s, cols_s, ident_d, mask_d, yT_s):
    p = P(nc, "l1b")
    S = p.S
    ident = p.sb([128, 128], BF16, "ident")
    identf = p.sb([128, 128], F32, "identf")
    mask = p.sb([128, 128], F32, "mask")
    cols = p.sb([128, 4, 32, 2], F32, "cols")
    cb = Buf("const")
    dma(S, "sp", identf[:], ident_d[:, :], writes=[cb])
    dma(S, "sp", mask[:], mask_d[:, :], writes=[cb])
    for i in range(4):
        dma(S, "sp", cols[:, i, :, :].rearrange("p n h -> p (n h)"), cols_s[i], writes=[cb])
    S.op("dve", lambda e: e.tensor_copy(out=ident[:], in_=identf[:]), reads=[cb], writes=[cb])
    qT = p.rot_sb(2, [128, 4, 128], BF16, "qT")
    kT = p.rot_sb(2, [128, 4, 128], BF16, "kT")
    Va = p.rot_sb(2, [128, 2, 257], BF16, "Va")
    ot = p.rot_sb(2, [128, 512], F32, "ot")
    Ka = p.rot_sb(2, [128, 512], BF16, "Ka")
    PT = p.rot_sb(2, [128, 2, 128], BF16, "PT")
    C32 = p.sb([128, 2, 2, 257], F32, "C32")
    Cbf = p.sb([128, 2, 2, 257], BF16, "Cbf")
    c32b = [[Buf() for _ in range(2)] for _ in range(4)]
    cbfb = [[Buf() for _ in range(2)] for _ in range(4)]
    stt = p.rot_sb(2, [128, 24], F32, "st")
    yb = p.rot_sb(2, [128, 512], BF16, "yb")
    yTt = p.rot_sb(2, [128, 4, 128], BF16, "yTt")
    psT = p.rot_ps(2, [128, 8, 128], BF16, "psT")
    psF = p.rot_ps(6, [128, 512], F32, "psF")
    for j in range(2):
        t_, b_ = Va.t[j], Va.b[j]
        S.op("pool", lambda e, t_=t_: e.memset(t_[:, :, 256:257], 1.0), writes=[b_])
    def do_tile(n):
        cs_ = slice(n * 128, (n + 1) * 128)
        q, qb_ = qT.next()
        k, kb_ = kT.next()
        va, vab = Va.next()
        o, ob = ot.next()
        dma(S, "sp", q[:], qkT_s[0:512, cs_].rearrange("(c p) t -> p c t", p=128), writes=[qb_])
        dma(S, "sp", k[:], qkT_s[512:1024, cs_].rearrange("(c p) t -> p c t", p=128), writes=[kb_])
        dma(S, "sp", va[:, :, 0:256], v_s[cs_, :].rearrange("t (h e) -> t h e", h=2), writes=[vab])
        dma(S, "sp", o[:], o_s[cs_, :], writes=[ob])
        pt, ptb = psT.next()
        for c in range(4):
            S.op("pe", lambda e, c=c, pt=pt, k=k: e.transpose(out=pt[:, c, :], in_=k[:, c, :], identity=ident[:]), reads=[kb_, cb], writes=[ptb])
        ka, kab = Ka.next()
        for h in range(2):
            S.op("dve", lambda e, h=h, pt=pt, ka=ka: e.tensor_scalar(out=ka[:, h * 256:(h + 1) * 256], in0=pt[:, 2 * h:2 * h + 2, :].rearrange("p c d -> p (c d)"),
                                                                   scalar1=cols[:, 0, n, h:h + 1], scalar2=None, op0=ALU.mult),
                 reads=[ptb, cb], writes=[kab])
        pss, pssb = psF.next()
        for h in range(2):
            for dc in range(2):
                S.op("pe", lambda e, h=h, dc=dc, pss=pss, k=k, q=q: e.matmul(pss[:, h * 128:(h + 1) * 128], lhsT=k[:, 2 * h + dc, :], rhs=q[:, 2 * h + dc, :],
                                                                          start=(dc == 0), stop=(dc == 1)), reads=[kb_, qb_], writes=[pssb])
        ptt, pttb = PT.next()
        for h in range(2):
            S.op("dve", lambda e, h=h, pss=pss, ptt=ptt: e.scalar_tensor_tensor(out=ptt[:, h, :], in0=pss[:, h * 128:(h + 1) * 128], scalar=cols[:, 0, n, h:h + 1],
                                                                              in1=mask[:], op0=ALU.mult, op1=ALU.mult), reads=[pssb, cb], writes=[pttb])
        st, stb = stt.next()
        pso = []
        for h in range(2):
            po, pob = psF.next()
            S.op("pe", lambda e, h=h, po=po, ptt=ptt, va=va: e.matmul(po[:, 0:257], lhsT=ptt[:, h, :], rhs=va[:, h, :], start=True, stop=(n == 0)),
                 reads=[pttb, vab], writes=[pob])
            if n > 0:
                for dc in range(2):
                    S.op("pe", lambda e, h=h, dc=dc, po=po, q=q: e.matmul(po[:, 0:257], lhsT=q[:, 2 * h + dc, :], rhs=Cbf[:, h, dc, :], start=False, stop=(dc == 1)),
                         reads=[qb_, cbfb[h][dc]], writes=[pob])
            S.op("dve", lambda e, h=h, po=po, st=st: e.tensor_copy(out=st[:, 16 + h:17 + h], in_=po[:, 256:257]), reads=[pob], writes=[stb])
            pso.append((po, pob))
        S.op("dve", lambda e, st=st: e.tensor_scalar(out=st[:, 20:22], in0=st[:, 16:18], scalar1=-1.0, scalar2=None, op0=ALU.mult), reads=[stb], writes=[stb])
        S.op("dve", lambda e, st=st: e.tensor_tensor(out=st[:, 20:22], in0=st[:, 20:22], in1=st[:, 16:18], op=ALU.max), reads=[stb], writes=[stb])
        S.op("dve", lambda e, st=st: e.tensor_tensor(out=st[:, 0:2], in0=st[:, 20:22], in1=cols[:, 1, n, :], op=ALU.mult), reads=[stb, cb], writes=[stb])
        S.op("dve", lambda e, st=st: e.tensor_tensor(out=st[:, 4:6], in0=st[:, 0:2], in1=cols[:, 2, n, :], op=ALU.max), reads=[stb, cb], writes=[stb])
        S.op("dve", lambda e, st=st: e.reciprocal(out=st[:, 8:10], in_=st[:, 4:6]), reads=[stb], writes=[stb])
        S.op("dve", lambda e, st=st: e.tensor_tensor(out=st[:, 12:14], in0=st[:, 8:10], in1=cols[:, 1, n, :], op=ALU.mult), reads=[stb, cb], writes=[stb])
        y, ybuf = yb.next()
        for h in range(2):
            po, pob = pso[h]
            S.op("dve", lambda e, h=h, po=po, y=y, st=st, o=o: e.scalar_tensor_tensor(out=y[:, h * 256:(h + 1) * 256], in0=po[:, 0:256], scalar=st[:, 12 + h:13 + h],
                                                                                    in1=o[:, h * 256:(h + 1) * 256], op0=ALU.mult, op1=ALU.mult),
                 reads=[pob, stb, ob], writes=[ybuf])
        if n < NT - 1:
            for h in range(2):
                for dc in range(2):
                    pk, pkb = psF.next()
                    S.op("pe", lambda e, h=h, dc=dc, pk=pk, ka=ka, va=va: e.matmul(pk[:, 0:257], lhsT=ka[:, h * 256 + dc * 128:h * 256 + (dc + 1) * 128], rhs=va[:, h, :],
                                                                                  start=True, stop=True), reads=[kab, vab], writes=[pkb])
                    if n == 0:
                        S.op("act", lambda e, h=h, dc=dc, pk=pk: e.copy(out=C32[:, h, dc, :], in_=pk[:, 0:257]), reads=[pkb], writes=[c32b[h][dc]])
                    else:
                        S.op("dve", lambda e, h=h, dc=dc, pk=pk: e.scalar_tensor_tensor(out=C32[:, h, dc, :], in0=C32[:, h, dc, :], scalar=cols[:, 3, n - 1, h:h + 1],
                                                                                      in1=pk[:, 0:257], op0=ALU.mult, op1=ALU.add),
                             reads=[pkb, c32b[h][dc], cb], writes=[c32b[h][dc]])
                    S.op("act", lambda e, h=h, dc=dc: e.activation(out=Cbf[:, h, dc, :], in_=C32[:, h, dc, :], func=AF.Identity, scale=cols[:, 3, n, h:h + 1]),
                         reads=[c32b[h][dc], cb], writes=[cbfb[h][dc]])
        yt, ytb = yTt.next()
        pt2, pt2b = psT.next()
        for c in range(4):
            S.op("pe", lambda e, c=c, pt2=pt2, y=y: e.transpose(out=pt2[:, c, :], in_=y[:, c * 128:(c + 1) * 128], identity=ident[:]), reads=[ybuf, cb], writes=[pt2b])
        S.op("act", lambda e, pt2=pt2, yt=yt: e.copy(out=yt[:], in_=pt2[:, 0:4, :]), reads=[pt2b], writes=[ytb])
        dma(S, "pool", yT_s[:, cs_].rearrange("(c p) t -> p c t", p=128), yt[:], reads=[ytb], writes=[Buf()])

    for n in range(NT):
        do_tile(n)
    p.finish()


def make_consts():
    ident = np.eye(128, dtype=np.float32)
    mask = (np.arange(128)[:, None] <= np.arange(128)[None, :]).astype(np.float32)
    d = 128
    inv = 1.0 / (10000.0 ** (np.arange(0, d, 2, dtype=np.float64) / d))
    ang = np.arange(S_LEN, dtype=np.float64)[:, None] * inv[None, :]
    cos, sin = np.cos(ang), np.sin(ang)
    pos = np.arange(S_LEN) % 128
    rot = np.zeros((4, S_LEN, 4, 64), np.float64)
    for h in range(4):
        lg = np.log1p(-2.0 ** (-5.0 - h))
        a = np.exp(lg * (pos + 1.0))[:, None]
        b = np.exp(-lg * (pos + 1.0))[:, None] * (d ** -0.5)
        rot[0, :, h] = cos * a
        rot[1, :, h] = sin * a
        rot[2, :, h] = cos * b
        rot[3, :, h] = sin * b
    return dict(ident=ident, mask=mask, rot=rot.reshape(4, S_LEN, 256).astype(np.float32))


def t5_bucket_np(dist):
    exact = 16
    n = np.maximum(dist, 0)
    large = exact + (np.log(np.maximum(n, 1).astype(np.float32) / exact) / math.log(2048 / exact) * (32 - exact)).astype(np.int32)
    large = np.minimum(large, 31)
    return np.where(n < exact, n, large)


def make_bm(rel_bias):
    k = np.arange(128)[:, None]
    q = np.arange(128)[None, :]
    out = np.zeros((8, 128, 3, 256), np.float32)
    for i, r in enumerate((1, 4, 16)):
        dcur = q - k
        dprev = q - k + 128
        for dist, c0 in ((dcur, 0), (dprev, 128)):
            valid = (dist >= 0) & (dist <= 128)
            idx = t5_bucket_np(dist * r)
            b = rel_bias[idx]
            b = np.where(valid[:, :, None], b, np.float32(-30000.0))
            out[:, :, i, c0:c0 + 128] = b.transpose(2, 0, 1)
    return out


PAIRS = [[0, 1], [2, 3], [4, 5], [6, 7]]
HALF = S_LEN // 2


def phase_allgather(nc, src, dst, nrows):
    nch = nrows // 256
    sems = []
    for j in range(nch):
        Sched.uid += 1
        sems.append(nc.alloc_semaphore(name=f"cc{Sched.uid}"))
    with nc.Block() as block:
        @block.gpsimd
        def _(g):
            for j in range(nch):
                g.collective_compute("AllGather", ALU.bypass, replica_groups=PAIRS, ins=[src[j * 256:(j + 1) * 256, :]],
                                     outs=[dst[j * 512:(j + 1) * 512, :]]).then_inc(sems[j])
                g.wait_ge(sems[j], 1)
    nc.all_engine_barrier()
    nc.clear_and_free_semaphores(sems)
    nc.all_engine_barrier()


def phase_allgather_small(nc, src, dst):
    Sched.uid += 1
    sem = nc.alloc_semaphore(name=f"cc{Sched.uid}")
    with nc.Block() as block:
        @block.gpsimd
        def _(g):
            g.collective_compute("AllGather", ALU.bypass, replica_groups=PAIRS, ins=[src], outs=[dst]).then_inc(sem)
            g.wait_ge(sem, 1)
    nc.all_engine_barrier()
    nc.clear_and_free_semaphores([sem])
    nc.all_engine_barrier()


def build_program():
    nc = bass.Bass("TRN2", target_bir_lowering=False)

    def din(name, shape, dt=F32):
        return nc.dram_tensor(name, list(shape), dt, kind="ExternalInput").ap()

    def scr(name, shape, dt=F32):
        return nc.dram_tensor(name, list(shape), dt, kind="Internal").ap()

    x = din("x", [S_LEN, D])
    xh = din("xh", [HALF, D])
    w_in0 = din("w_in0", [D, 1792])
    w_out0 = din("w_out0", [D, D])
    rot = din("rot", [4, S_LEN, 128])
    ident = din("ident", [128, 128])
    mask = din("mask", [128, 128])
    e127 = din("e127", [128, 128])
    retg = din("retg", [128, 2])
    sel = din("sel", [128, 2])
    bm = din("bm", [4, 128, 3, 256])
    ln_g = din("ln_g", [2, 2, D])
    ln_b = din("ln_b", [2, 2, D])
    w_up = [din(f"w_up{l}", [D, 2 * D_FF]) for l in range(2)]
    w_dn = [din(f"w_dn{l}", [D_FF, D]) for l in range(2)]
    cw = [din(f"cw{l}", [128, NFC, 3]) for l in range(2)]
    cbv = [din(f"cb{l}", [128, NFC]) for l in range(2)]
    w_in1 = din("w_in1", [D, 2052])
    w_out1 = din("w_out1", [D, D])
    cwm = din("cwm", [128, 8, 4])
    gb = din("gb", [2, 2])
    out = nc.dram_tensor("out", [HALF, D], F32, kind="ExternalOutput").ap()

    yT_loc = scr("yT_loc", [512, S_LEN], BF16)
    yT_g = scr("yT_g", [1024, S_LEN], BF16)
    qkdT_s = scr("qkdT_s", [512, S_LEN], BF16)
    vd_s = scr("vd_s", [S_LEN, 256], BF16)
    x1 = scr("x1", [HALF, D])
    x1T = scr("x1T", [D, HALF], BF16)
    x2 = scr("x2", [HALF, D])
    x2T = scr("x2T", [D, HALF], BF16)
    x2T_g = scr("x2T_g", [2 * D, HALF], BF16)
    qkT_s = scr("qkT_s", [1024, S_LEN], BF16)
    v_s = scr("v_s", [S_LEN, 512], BF16)
    o_s = scr("o_s", [S_LEN, 512])
    gi_s = scr("gi_s", [2, S_LEN])
    gf_s = scr("gf_s", [2, S_LEN])
    cols_s = scr("cols_s", [4, 128, 64])
    x3 = scr("x3", [HALF, D])
    x3T = scr("x3T", [D, HALF], BF16)
    x4T = scr("x4T", [D, HALF], BF16)
    xhT0 = scr("xhT0", [D, 128], BF16)
    xhT1 = scr("xhT1", [D, 128], BF16)
    x2h = scr("x2h", [128, D])
    x2h_g = scr("x2h_g", [256, D])

    NP = int(os.environ.get("NPHASE", "99"))
    phases = [
        lambda: phase_l0a(nc, x, w_in0, rot, ident, mask, retg, yT_loc, qkdT_s, vd_s),
        lambda: phase_l0b(nc, qkdT_s, vd_s, bm, yT_loc),
        lambda: phase_allgather(nc, yT_loc, yT_g, 512),
        lambda: phase_outproj(nc, "l0c", xh, yT_g, sel, w_out0, ln_g[0, 0], ln_b[0, 0], ident, x1, x1T, halo_x=x[HALF - 128:HALF, :], xhT_out=xhT0),
        lambda: phase_ffn(nc, "l0d", x1, x1T, w_up[0], cw[0], cbv[0], w_dn[0], ln_g[0, 1], ln_b[0, 1], ident, x2, x2T, ntiles=16, xhT_in=xhT0, sel_d=sel),
        lambda: phase_allgather(nc, x2T, x2T_g, 1024),
        lambda: phase_allgather_small(nc, x2[HALF - 128:HALF, :], x2h_g),
        lambda: phase_l1a(nc, x2T_g, w_in1, cwm, qkT_s, v_s, o_s, gi_s, gf_s),
        lambda: phase_l1p(nc, gi_s, gf_s, gb, ident, e127, cols_s),
        lambda: phase_l1b(nc, qkT_s, v_s, o_s, cols_s, ident, mask, yT_loc),
        lambda: phase_allgather(nc, yT_loc, yT_g, 512),
        lambda: phase_outproj(nc, "l1c", x2, yT_g, sel, w_out1, ln_g[1, 0], ln_b[1, 0], ident, x3, x3T, halo_x=x2h_g[0:128, :], xhT_out=xhT1),
        lambda: phase_ffn(nc, "l1d", x3, x3T, w_up[1], cw[1], cbv[1], w_dn[1], ln_g[1, 1], ln_b[1, 1], ident, out, x4T, ntiles=16, xhT_in=xhT1, sel_d=sel),
    ]
    for ph in phases[:NP]:
        ph()
    return nc


def kernel(x, even_w_in, even_w_out, rel_bias, odd_w_in, odd_gate_b, odd_conv_w, odd_w_out,
           ffn_w_up, ffn_conv_w, ffn_conv_b, ffn_w_down, ln_g, ln_b):
    f = lambda a: np.ascontiguousarray(np.asarray(a, dtype=np.float32))
    c = make_consts()
    e127 = np.zeros((128, 128), np.float32)
    e127[127, :] = 1.0
    x = np.asarray(x, dtype=np.float32)
    nb = x.shape[0]
    w0 = np.asarray(even_w_in[0], dtype=np.float32)
    w1 = np.asarray(odd_w_in[0], dtype=np.float32)
    wo0 = np.asarray(even_w_out[0], dtype=np.float32)
    bm_all = make_bm(f(rel_bias))
    cwm_all = np.asarray(odd_conv_w[0], dtype=np.float32)
    gbv = np.asarray(odd_gate_b[0], dtype=np.float32)
    shared = dict(ident=c["ident"], mask=c["mask"], e127=e127, ln_g=f(ln_g), ln_b=f(ln_b), w_out1=f(odd_w_out[0]))
    for l in range(2):
        shared[f"w_up{l}"] = f(ffn_w_up[l])
        shared[f"w_dn{l}"] = f(ffn_w_down[l])
        shared[f"cw{l}"] = f(np.asarray(ffn_conv_w[l]).reshape(3, NFC, 128).transpose(2, 1, 0))
        shared[f"cb{l}"] = f(np.asarray(ffn_conv_b[l]).reshape(NFC, 128).T)
    per_m = []
    for m in range(2):
        r0, d0 = m * 256, m * 256
        cols0 = np.concatenate([np.arange(r0, r0 + 256), 512 + np.arange(r0, r0 + 256), 1024 + np.arange(r0, r0 + 256), 1536 + np.arange(r0, r0 + 256),
                                2048 + np.arange(d0, d0 + 256), 2560 + np.arange(d0, d0 + 256), 3072 + np.arange(d0, d0 + 256)])
        h0 = m * 512
        cols1 = np.concatenate([np.arange(h0, h0 + 512), 1024 + np.arange(h0, h0 + 512), 2048 + np.arange(h0, h0 + 512), 3072 + np.arange(h0, h0 + 512),
                                4096 + np.arange(2 * m, 2 * m + 2), 4100 + np.arange(2 * m, 2 * m + 2)])
        qkf = np.concatenate([np.arange(h0, h0 + 512), 1024 + np.arange(h0, h0 + 512)])
        retg = np.zeros((128, 2), np.float32)
        retg[:, 0] = RET_G[2 * m]
        retg[:, 1] = RET_G[2 * m + 1]
        sel = np.zeros((128, 2), np.float32)
        sel[:, m] = 1.0
        per_m.append(dict(
            w_in0=f(w0[:, cols0]), rot=f(c["rot"][:, :, m * 128:(m + 1) * 128]), retg=retg, sel=sel, bm=f(bm_all[4 * m:4 * m + 4]),
            w_in1=f(w1[:, cols1]), cwm=f(cwm_all[:, qkf].reshape(4, 8, 128).transpose(2, 1, 0)),
            gb=f(np.stack([gbv[2 * m:2 * m + 2], gbv[4 + 2 * m:4 + 2 * m + 2]], axis=1)),
        ))
    perm = np.concatenate([np.arange(0, 256), np.arange(512, 768), np.arange(256, 512), np.arange(768, 1024)])
    shared["w_out0"] = f(wo0)
    shared["w_out1"] = f(np.asarray(odd_w_out[0], dtype=np.float32)[perm])
    nc = build_program()
    in_maps = []
    for core in range(2 * nb):
        b, m = divmod(core, 2)
        in_maps.append(dict(shared, **per_m[m], x=np.ascontiguousarray(x[b]), xh=np.ascontiguousarray(x[b, m * HALF:(m + 1) * HALF])))
    res = run_bass_kernel_spmd(nc, in_maps, core_ids=list(range(2 * nb)))
    out = np.zeros((nb, S_LEN, D), np.float32)
    for core in range(2 * nb):
        b, m = divmod(core, 2)
        out[b, m * HALF:(m + 1) * HALF] = np.asarray(res.results[core]["out"], dtype=np.float32)
    return out
```

```python
import math, os
from contextlib import ExitStack
import numpy as np
import concourse.bass as bass
import concourse.mybir as mybir
from concourse.bass_utils import run_bass_kernel_spmd

F32 = mybir.dt.float32
BF16 = mybir.dt.bfloat16
AF = mybir.ActivationFunctionType
ALU = mybir.AluOpType
AX = mybir.AxisListType

S_LEN = 4096
D = 1024
NT = S_LEN // 128
NG = S_LEN // 512
D_FF = 2816
NFC = D_FF // 128
ALPHA = 4.0 ** 0.25
LN_EPS = 1e-5

COMPUTE = ("pe", "act", "dve", "pool")
QUEUES = ("sp", "act", "pool")
NSLOT = 8


class Buf:
    __slots__ = ("name", "w", "r", "rd")

    def __init__(self, name=""):
        self.name = name
        self.w = None
        self.r = {}
        self.rd = []


def bufs(name, n):
    return [Buf(f"{name}{i}") for i in range(n)]


class Op:
    __slots__ = ("eng", "fn", "deps", "dma", "sig", "val", "sem", "k")

    def __init__(self, eng, fn, dma):
        self.eng = eng
        self.fn = fn
        self.dma = dma
        self.deps = []
        self.sig = False
        self.val = 0
        self.sem = None
        self.k = 0


class Sched:
    uid = 0

    def __init__(self):
        self.ops = {e: [] for e in ("pe", "act", "dve", "pool", "sp")}

    def op(self, eng, fn, reads=(), writes=(), dma=False):
        o = Op(eng, fn, dma)
        deps = {}

        def add(p, raw):
            if p is None:
                return
            if not p.dma and not dma and p.eng == eng:
                if not raw or eng == "pe":
                    return
            deps[id(p)] = p

        for b in reads:
            add(b.w, True)
        for b in writes:
            add(b.w, False)
            for p in b.r.values():
                add(p, False)
            for p in b.rd:
                add(p, False)
        o.deps = list(deps.values())
        for p in o.deps:
            p.sig = True
        ws = set(id(b) for b in writes)
        for b in writes:
            b.w = o
            b.r = {}
            b.rd = []
        for b in reads:
            if id(b) in ws:
                continue
            if dma:
                b.rd.append(o)
            else:
                b.r[eng] = o
        self.ops[eng].append(o)
        return o

    def emit(self, nc, es):
        Sched.uid += 1
        u = Sched.uid
        csem = {e: nc.alloc_semaphore(name=f"c{u}_{e}") for e in COMPUTE}
        qsem = {q: [nc.alloc_semaphore(name=f"q{u}_{q}{i}") for i in range(NSLOT)] for q in QUEUES}
        self.sems = list(csem.values()) + [s_ for q in QUEUES for s_ in qsem[q]]
        for e in COMPUTE:
            c = 0
            for o in self.ops[e]:
                if o.dma:
                    continue
                if o.sig:
                    c += 1
                    o.val = c
                    o.sem = csem[e]
        dmas = {q: [] for q in QUEUES}
        for q in QUEUES:
            k = 0
            for o in self.ops[q]:
                if not o.dma:
                    continue
                o.k = k
                o.sem = qsem[q][k % NSLOT]
                o.val = 16 * (k // NSLOT + 1)
                dmas[q].append(o)
                k += 1
        ops = self.ops

        def run(eng_name, eng):
            waited = {}

            def wait(sem, val):
                key = id(sem)
                if waited.get(key, 0) >= val:
                    return
                waited[key] = val
                eng.wait_ge(sem, val)

            for o in ops[eng_name]:
                for p in o.deps:
                    wait(p.sem, p.val)
                if o.dma and o.k >= NSLOT:
                    prev = dmas[eng_name][o.k - NSLOT]
                    wait(prev.sem, prev.val)
                ins = o.fn(eng)
                if o.dma:
                    ins.then_inc(o.sem, 16)
                elif o.sig:
                    ins.then_inc(o.sem, 1)
            if eng_name in dmas:
                for o in dmas[eng_name][-NSLOT:]:
                    wait(o.sem, o.val)

        with nc.Block() as block:
            @block.tensor
            def _(e):
                run("pe", e)

            @block.scalar
            def _(e):
                run("act", e)

            @block.vector
            def _(e):
                run("dve", e)

            @block.gpsimd
            def _(e):
                run("pool", e)

            @block.sync
            def _(e):
                run("sp", e)


class Rot:
    def __init__(self, tiles):
        self.t = tiles
        self.b = bufs("rot", len(tiles))
        self.i = 0

    def next(self):
        i = self.i % len(self.t)
        self.i += 1
        return self.t[i], self.b[i]


class P:
    def __init__(self, nc, name):
        self.nc = nc
        self.name = name
        self.es = ExitStack()
        self.S = Sched()
        self.n = 0

    def sb(self, shape, dt, name=None):
        self.n += 1
        return self.es.enter_context(self.nc.sbuf_tensor(f"{self.name}_{name or 't'}{self.n}", list(shape), dt))

    def ps(self, shape, dt, name=None):
        self.n += 1
        return self.es.enter_context(self.nc.psum_tensor(f"{self.name}_{name or 'p'}{self.n}", list(shape), dt))

    def rot_sb(self, n, shape, dt, name=None):
        return Rot([self.sb(shape, dt, name) for _ in range(n)])

    def rot_ps(self, n, shape, dt, name=None):
        return Rot([self.ps(shape, dt, name) for _ in range(n)])

    def finish(self):
        self.S.emit(self.nc, self.es)
        self.nc.all_engine_barrier()
        self.nc.clear_and_free_semaphores(self.S.sems)
        self.nc.all_engine_barrier()
        self.es.close()


def dma(S, q, out, in_, reads=(), writes=()):
    return S.op(q, lambda e: e.dma_start(out=out, in_=in_), reads=reads, writes=writes, dma=True)


def load_weight(p, dst, dst_bufs, src, nchunk, cols, stage, cast_engs=("pool",), stage_cols=3584):
    S = p.S
    i = 0
    for k in range(nchunk):
        c0 = 0
        while c0 < cols:
            cw = min(stage_cols, cols - c0)
            st, sbuf = stage.next()
            dma(S, "sp", st[:, 0:cw], src[k * 128:(k + 1) * 128, c0:c0 + cw], writes=[sbuf])
            eng = cast_engs[i % len(cast_engs)]
            i += 1
            if eng == "act":
                S.op("act", lambda e, st=st, k=k, c0=c0, cw=cw: e.copy(out=dst[:, k, c0:c0 + cw], in_=st[:, 0:cw]),
                     reads=[sbuf], writes=[dst_bufs[k]])
            else:
                S.op(eng, lambda e, st=st, k=k, c0=c0, cw=cw: e.tensor_copy(out=dst[:, k, c0:c0 + cw], in_=st[:, 0:cw]),
                     reads=[sbuf], writes=[dst_bufs[k]])
            c0 += cw


def layer_norm_tile(p, z, zb, gB, bB, cbuf, outt, outb, sq, sqb, st, stb):
    S = p.S
    S.op("dve", lambda e: e.reduce_sum(out=st[:, 0:1], in_=z[:], axis=AX.X), reads=[zb], writes=[stb])
    S.op("dve", lambda e: e.tensor_tensor(out=sq[:], in0=z[:], in1=z[:], op=ALU.mult), reads=[zb], writes=[sqb])
    S.op("dve", lambda e: e.reduce_sum(out=st[:, 1:2], in_=sq[:], axis=AX.X), reads=[sqb], writes=[stb])
    S.op("dve", lambda e: e.tensor_scalar(out=st[:, 2:3], in0=st[:, 0:1], scalar1=1.0 / D, scalar2=None, op0=ALU.mult),
         reads=[stb], writes=[stb])
    S.op("dve", lambda e: e.tensor_tensor(out=st[:, 3:4], in0=st[:, 2:3], in1=st[:, 2:3], op=ALU.mult),
         reads=[stb], writes=[stb])
    S.op("dve", lambda e: e.scalar_tensor_tensor(out=st[:, 4:5], in0=st[:, 1:2], scalar=1.0 / D, in1=st[:, 3:4],
                                                 op0=ALU.mult, op1=ALU.subtract), reads=[stb], writes=[stb])
    S.op("dve", lambda e: e.tensor_scalar(out=st[:, 6:7], in0=st[:, 4:5], scalar1=LN_EPS, scalar2=None, op0=ALU.add),
         reads=[stb], writes=[stb])
    S.op("act", lambda e: e.activation(out=st[:, 7:8], in_=st[:, 6:7], func=AF.Sqrt), reads=[stb], writes=[stb])
    S.op("dve", lambda e: e.reciprocal(out=st[:, 5:6], in_=st[:, 7:8]), reads=[stb], writes=[stb])
    S.op("dve", lambda e: e.tensor_scalar(out=sq[:], in0=z[:], scalar1=st[:, 2:3], scalar2=st[:, 5:6],
                                          op0=ALU.subtract, op1=ALU.mult), reads=[zb, stb], writes=[sqb])
    S.op("dve", lambda e: e.tensor_tensor(out=sq[:], in0=sq[:], in1=gB[:], op=ALU.mult), reads=[sqb, cbuf], writes=[sqb])
    S.op("dve", lambda e: e.tensor_tensor(out=outt[:], in0=sq[:], in1=bB[:], op=ALU.add), reads=[sqb, cbuf], writes=[outb])


def layer_norm_multi(p, tiles, gB, bB, cbuf):
    S = p.S

    def steps_for(z, zb, outt, outb, sq, sqb, st, stb):
        return [
            lambda: S.op("dve", lambda e: e.reduce_sum(out=st[:, 0:1], in_=z[:], axis=AX.X), reads=[zb], writes=[stb]),
            lambda: S.op("dve", lambda e: e.tensor_tensor(out=sq[:], in0=z[:], in1=z[:], op=ALU.mult), reads=[zb], writes=[sqb]),
            lambda: S.op("dve", lambda e: e.reduce_sum(out=st[:, 1:2], in_=sq[:], axis=AX.X), reads=[sqb], writes=[stb]),
            lambda: S.op("dve", lambda e: e.tensor_scalar(out=st[:, 2:3], in0=st[:, 0:1], scalar1=1.0 / D, scalar2=None, op0=ALU.mult),
                         reads=[stb], writes=[stb]),
            lambda: S.op("dve", lambda e: e.tensor_tensor(out=st[:, 3:4], in0=st[:, 2:3], in1=st[:, 2:3], op=ALU.mult),
                         reads=[stb], writes=[stb]),
            lambda: S.op("dve", lambda e: e.scalar_tensor_tensor(out=st[:, 4:5], in0=st[:, 1:2], scalar=1.0 / D, in1=st[:, 3:4],
                                                                 op0=ALU.mult, op1=ALU.subtract), reads=[stb], writes=[stb]),
            lambda: S.op("dve", lambda e: e.tensor_scalar(out=st[:, 6:7], in0=st[:, 4:5], scalar1=LN_EPS, scalar2=None, op0=ALU.add),
                         reads=[stb], writes=[stb]),
            lambda: S.op("act", lambda e: e.activation(out=st[:, 7:8], in_=st[:, 6:7], func=AF.Sqrt), reads=[stb], writes=[stb]),
            lambda: S.op("dve", lambda e: e.reciprocal(out=st[:, 5:6], in_=st[:, 7:8]), reads=[stb], writes=[stb]),
            lambda: S.op("dve", lambda e: e.tensor_scalar(out=sq[:], in0=z[:], scalar1=st[:, 2:3], scalar2=st[:, 5:6],
                                                          op0=ALU.subtract, op1=ALU.mult), reads=[zb, stb], writes=[sqb]),
            lambda: S.op("dve", lambda e: e.tensor_tensor(out=sq[:], in0=sq[:], in1=gB[:], op=ALU.mult), reads=[sqb, cbuf], writes=[sqb]),
            lambda: S.op("dve", lambda e: e.tensor_tensor(out=outt[:], in0=sq[:], in1=bB[:], op=ALU.add), reads=[sqb, cbuf], writes=[outb]),
        ]

    lists = [steps_for(*tl) for tl in tiles]
    for i in range(len(lists[0])):
        for lst in lists:
            lst[i]()


def layer_norm_pair_inplace(p, tiles, sq, sqb, gB, bB, cbuf):
    S = p.S
    for (z, zb, st, stb) in tiles:
        S.op("dve", lambda e, z=z, st=st: e.reduce_sum(out=st[:, 0:1], in_=z[:], axis=AX.X), reads=[zb], writes=[stb])
        S.op("dve", lambda e, z=z: e.tensor_tensor(out=sq[:], in0=z[:], in1=z[:], op=ALU.mult), reads=[zb], writes=[sqb])
        S.op("dve", lambda e, st=st: e.reduce_sum(out=st[:, 1:2], in_=sq[:], axis=AX.X), reads=[sqb], writes=[stb])

    def steps_for(z, zb, st, stb):
        return [
            lambda: S.op("dve", lambda e: e.tensor_scalar(out=st[:, 2:3], in0=st[:, 0:1], scalar1=1.0 / D, scalar2=None, op0=ALU.mult),
                         reads=[stb], writes=[stb]),
            lambda: S.op("dve", lambda e: e.tensor_tensor(out=st[:, 3:4], in0=st[:, 2:3], in1=st[:, 2:3], op=ALU.mult),
                         reads=[stb], writes=[stb]),
            lambda: S.op("dve", lambda e: e.scalar_tensor_tensor(out=st[:, 4:5], in0=st[:, 1:2], scalar=1.0 / D, in1=st[:, 3:4],
                                                                 op0=ALU.mult, op1=ALU.subtract), reads=[stb], writes=[stb]),
            lambda: S.op("dve", lambda e: e.tensor_scalar(out=st[:, 6:7], in0=st[:, 4:5], scalar1=LN_EPS, scalar2=None, op0=ALU.add),
                         reads=[stb], writes=[stb]),
            lambda: S.op("act", lambda e: e.activation(out=st[:, 7:8], in_=st[:, 6:7], func=AF.Sqrt), reads=[stb], writes=[stb]),
            lambda: S.op("dve", lambda e: e.reciprocal(out=st[:, 5:6], in_=st[:, 7:8]), reads=[stb], writes=[stb]),
            lambda: S.op("dve", lambda e: e.tensor_scalar(out=z[:], in0=z[:], scalar1=st[:, 2:3], scalar2=st[:, 5:6],
                                                          op0=ALU.subtract, op1=ALU.mult), reads=[zb, stb], writes=[zb]),
            lambda: S.op("dve", lambda e: e.tensor_tensor(out=z[:], in0=z[:], in1=gB[:], op=ALU.mult), reads=[zb, cbuf], writes=[zb]),
            lambda: S.op("dve", lambda e: e.tensor_tensor(out=z[:], in0=z[:], in1=bB[:], op=ALU.add), reads=[zb, cbuf], writes=[zb]),
        ]

    lists = [steps_for(*tl) for tl in tiles]
    for i in range(len(lists[0])):
        for lst in lists:
            lst[i]()


def transpose_tile(p, src, srcb, ident, identb, psT, dstT_view, dstb, nchunk=8, evac="act"):
    S = p.S
    pt, ptb = psT.next()
    for c in range(nchunk):
        S.op("pe", lambda e, c=c, pt=pt: e.transpose(out=pt[:, c, :], in_=src[:, c * 128:(c + 1) * 128], identity=ident[:]),
             reads=[srcb, identb], writes=[ptb])
    if evac == "act":
        S.op("act", lambda e, pt=pt: e.copy(out=dstT_view, in_=pt[:, 0:nchunk, :]), reads=[ptb], writes=[dstb])
    else:
        S.op(evac, lambda e, pt=pt: e.tensor_copy(out=dstT_view, in_=pt[:, 0:nchunk, :]), reads=[ptb], writes=[dstb])


RET_G = [(1.0 - 2.0 ** (-5.0 - h)) ** 128 for h in range(4)]


def phase_l0a(nc, x, w_in, rot, ident_d, mask_d, retg_d, yT_s, qkdT_s, vd_s, ntiles=NT):
    p = P(nc, "l0a")
    S = p.S
    Win = p.sb([128, 8, 1792], BF16, "win")
    winb = bufs("win", 8)
    stage = p.rot_sb(4, [128, 1792], F32, "stage")
    ident = p.sb([128, 128], BF16, "ident")
    identf = p.sb([128, 128], F32, "identf")
    mask4 = p.sb([128, 2, 128], F32, "mask4")
    retg = p.sb([128, 2], F32, "retg")
    cb = Buf("const")
    dma(S, "sp", identf[:], ident_d[:, :], writes=[cb])
    for h in range(2):
        dma(S, "sp", mask4[:, h, :], mask_d[:, :], writes=[cb])
    dma(S, "sp", retg[:], retg_d[:, :], writes=[cb])
    S.op("dve", lambda e: e.tensor_copy(out=ident[:], in_=identf[:]), reads=[cb], writes=[cb])
    load_weight(p, Win, winb, w_in, 8, 1792, stage, cast_engs=("act", "dve"))

    xf = p.rot_sb(2, [128, 1024], F32, "xf")
    xb = p.rot_sb(2, [128, 1024], BF16, "xb")
    xT = p.rot_sb(2, [128, 8, 512], BF16, "xT")
    tab = p.rot_sb(2, [128, 4, 128], F32, "tab")
    psT = p.rot_ps(2, [128, 8, 128], BF16, "psT")
    psF = p.rot_ps(6, [128, 512], F32, "psF")
    tmp = p.rot_sb(4, [128, 2, 64], F32, "tmp")
    Qa = p.rot_sb(3, [128, 256], BF16, "Qa")
    Kb = p.rot_sb(3, [128, 256], BF16, "Kb")
    V = p.rot_sb(3, [128, 256], BF16, "V")
    G = p.rot_sb(3, [128, 256], F32, "G")
    Vd = p.rot_sb(2, [128, 256], BF16, "Vd")
    QKT = p.rot_sb(2, [128, 4, 128], BF16, "QKT")
    PT = p.rot_sb(2, [128, 2, 128], BF16, "PT")
    R32 = p.sb([128, 2, 128], F32, "R32")
    Rbf = p.sb([128, 2, 128], BF16, "Rbf")
    r32b, rbfb = Buf("r32"), Buf("rbf")
    sqt = p.rot_sb(2, [128, 256], F32, "sqt")
    ysr = p.rot_sb(2, [128, 256], F32, "ysr")
    stt = p.rot_sb(2, [128, 32], F32, "stt")
    yc = p.rot_sb(2, [128, 2, 128], F32, "yc")
    yg = p.rot_sb(3, [128, 256], BF16, "yg")
    yTt = p.rot_sb(2, [128, 2, 128], BF16, "yTt")
    qke = p.rot_sb(2, [128, 512], BF16, "qke")
    ytb = bufs("yTs", 1)[0]
    qkb, vdb = Buf("qkd"), Buf("vds")

    ngroups = (ntiles + 3) // 4
    state = {}

    def stage_a(t):
        g, tt = divmod(t, 4)
        xTt, xTb = state[("xT", g)]
        cols = slice(tt * 128, (tt + 1) * 128)
        tb_, tbb = tab.next()
        dma(S, "sp", tb_[:], rot[:, t * 128:(t + 1) * 128, :].rearrange("j p c -> p j c"), writes=[tbb])

        def proj(c0, wd=512):
            ps, psb = psF.next()
            for k in range(8):
                S.op("pe", lambda e, k=k, ps=ps: e.matmul(ps[:, 0:wd], lhsT=xTt[:, k, cols], rhs=Win[:, k, c0:c0 + wd],
                                                          start=(k == 0), stop=(k == 7)),
                     reads=[xTb, winb[k]], writes=[psb])
            return ps, psb

        def rotary(psv, psb, jc, js, dst, dstb):
            v = psv.rearrange("p (h d) -> p h d", h=2)
            x1 = v[:, :, 0:64]
            x2 = v[:, :, 64:128]
            cs = tb_[:, jc, :].rearrange("p (h d) -> p h d", h=2)
            sn = tb_[:, js, :].rearrange("p (h d) -> p h d", h=2)
            dv = dst[:].rearrange("p (h d) -> p h d", h=2)
            t1, t1b = tmp.next()
            t2, t2b = tmp.next()
            S.op("dve", lambda e: e.tensor_tensor(out=t1[:], in0=x1, in1=cs, op=ALU.mult), reads=[psb, tbb], writes=[t1b])
            S.op("dve", lambda e: e.tensor_tensor(out=t2[:], in0=x2, in1=sn, op=ALU.mult), reads=[psb, tbb], writes=[t2b])
            S.op("dve", lambda e: e.tensor_tensor(out=dv[:, :, 0:64], in0=t1[:], in1=t2[:], op=ALU.subtract),
                 reads=[t1b, t2b], writes=[dstb])
            t3, t3b = tmp.next()
            t4, t4b = tmp.next()
            S.op("dve", lambda e: e.tensor_tensor(out=t3[:], in0=x1, in1=sn, op=ALU.mult), reads=[psb, tbb], writes=[t3b])
            S.op("dve", lambda e: e.tensor_tensor(out=t4[:], in0=x2, in1=cs, op=ALU.mult), reads=[psb, tbb], writes=[t4b])
            S.op("dve", lambda e: e.tensor_tensor(out=dv[:, :, 64:128], in0=t3[:], in1=t4[:], op=ALU.add),
                 reads=[t3b, t4b], writes=[dstb])

        qa, qab = Qa.next()
        kb_, kbb = Kb.next()
        vv, vb = V.next()
        gg, gb = G.next()
        ps, psb = proj(0)
        rotary(ps[:, 0:256], psb, 0, 1, qa, qab)
        rotary(ps[:, 256:512], psb, 2, 3, kb_, kbb)
        ps, psb = proj(512)
        S.op("act", lambda e, ps=ps: e.copy(out=vv[:], in_=ps[:, 0:256]), reads=[psb], writes=[vb])
        S.op("act", lambda e, ps=ps: e.activation(out=gg[:], in_=ps[:, 256:512], func=AF.Silu), reads=[psb], writes=[gb])
        ps, psb = proj(1536, 256)
        vd, vdbuf = Vd.next()
        S.op("act", lambda e, ps=ps: e.copy(out=vd[:], in_=ps[:, 0:256]), reads=[psb], writes=[vdbuf])
        dma(S, "pool", vd_s[t * 128:(t + 1) * 128, :], vd[:], reads=[vdbuf], writes=[vdb])
        state[("t", t)] = (qa, qab, kb_, kbb, vv, vb, gg, gb)

    def stage_b(t):
        qa, qab, kb_, kbb, vv, vb, gg, gb = state.pop(("t", t))
        qkt, qktb = QKT.next()
        pt, ptb = psT.next()
        for h in range(2):
            S.op("pe", lambda e, h=h: e.transpose(out=pt[:, h, :], in_=qa[:, h * 128:(h + 1) * 128], identity=ident[:]),
                 reads=[qab, cb], writes=[ptb])
        for h in range(2):
            S.op("pe", lambda e, h=h: e.transpose(out=pt[:, 2 + h, :], in_=kb_[:, h * 128:(h + 1) * 128], identity=ident[:]),
                 reads=[kbb, cb], writes=[ptb])
        S.op("act", lambda e: e.copy(out=qkt[:], in_=pt[:, 0:4, :]), reads=[ptb], writes=[qktb])
        pss, pssb = psF.next()
        for h in range(2):
            S.op("pe", lambda e, h=h: e.matmul(pss[:, h * 128:(h + 1) * 128], lhsT=qkt[:, 2 + h, :], rhs=qkt[:, h, :],
                                               start=True, stop=True), reads=[qktb], writes=[pssb])
        ptt, pttb = PT.next()
        S.op("dve", lambda e: e.tensor_tensor(out=ptt[:].rearrange("p h c -> p (h c)"), in0=pss[:, 0:256],
                                              in1=mask4[:].rearrange("p h c -> p (h c)"), op=ALU.mult),
             reads=[pssb, cb], writes=[pttb])
        BL = int(os.environ.get("B_LEVEL", "9"))
        if BL < 2:
            return
        do_state = t < ntiles - 1
        if do_state:
            pskv, pskvb = psF.next()
            for h in range(2):
                hs = slice(h * 128, (h + 1) * 128)
                S.op("pe", lambda e, hs=hs: e.matmul(pskv[:, hs], lhsT=kb_[:, hs], rhs=vv[:, hs], start=True, stop=True),
                     reads=[kbb, vb], writes=[pskvb])
        psy, psyb = psF.next()
        for h in range(2):
            hs = slice(h * 128, (h + 1) * 128)
            S.op("pe", lambda e, h=h, hs=hs: e.matmul(psy[:, hs], lhsT=ptt[:, h, :], rhs=vv[:, hs], start=True, stop=(t == 0)),
                 reads=[pttb, vb], writes=[psyb])
            if t > 0:
                S.op("pe", lambda e, h=h, hs=hs: e.matmul(psy[:, hs], lhsT=qkt[:, h, :], rhs=Rbf[:, h, :], start=False, stop=True),
                     reads=[qktb, rbfb], writes=[psyb])
        if do_state:
            for h in range(2):
                hs = slice(h * 128, (h + 1) * 128)
                if t == 0:
                    S.op("dve", lambda e, h=h, hs=hs: e.tensor_copy(out=R32[:, h, :], in_=pskv[:, hs]), reads=[pskvb], writes=[r32b])
                else:
                    S.op("dve", lambda e, h=h, hs=hs: e.scalar_tensor_tensor(out=R32[:, h, :], in0=R32[:, h, :], scalar=retg[:, h:h + 1],
                                                                             in1=pskv[:, hs], op0=ALU.mult, op1=ALU.add),
                         reads=[pskvb, r32b, cb], writes=[r32b])
            for h in range(2):
                S.op("act", lambda e, h=h: e.activation(out=Rbf[:, h, :], in_=R32[:, h, :], func=AF.Identity, scale=retg[:, h:h + 1]),
                     reads=[r32b], writes=[rbfb])
        if BL < 4:
            return
        st, stb = stt.next()
        sq, sqb = sqt.next()
        ycc, ycb = yc.next()
        ygg, ygb = yg.next()
        hn_n = [0]
        hn_max = int(os.environ.get("HN_OPS", "99"))

        def HN(*a, **k):
            hn_n[0] += 1
            if hn_n[0] <= hn_max:
                S.op(*a, **k)
        ysb, ysbb = ysr.next()
        HN("act", lambda e: e.copy(out=ysb[:], in_=psy[:, 0:256]), reads=[psyb], writes=[ysbb])
        y4 = ysb[:].rearrange("p (h d) -> p h d", h=2)
        HN("dve", lambda e: e.reduce_sum(out=st[:, 0:2], in_=y4, axis=AX.X), reads=[ysbb], writes=[stb])
        HN("dve", lambda e: e.tensor_tensor(out=sq[:], in0=ysb[:], in1=ysb[:], op=ALU.mult), reads=[ysbb], writes=[sqb])
        HN("dve", lambda e: e.reduce_sum(out=st[:, 4:6], in_=sq[:].rearrange("p (h d) -> p h d", h=2), axis=AX.X),
             reads=[sqb], writes=[stb])
        HN("dve", lambda e: e.tensor_scalar(out=st[:, 8:10], in0=st[:, 0:2], scalar1=1.0 / 128, scalar2=None, op0=ALU.mult),
             reads=[stb], writes=[stb])
        HN("dve", lambda e: e.tensor_tensor(out=st[:, 12:14], in0=st[:, 8:10], in1=st[:, 8:10], op=ALU.mult),
             reads=[stb], writes=[stb])
        HN("dve", lambda e: e.scalar_tensor_tensor(out=st[:, 16:18], in0=st[:, 4:6], scalar=1.0 / 128, in1=st[:, 12:14],
                                                     op0=ALU.mult, op1=ALU.subtract), reads=[stb], writes=[stb])
        HN("dve", lambda e: e.tensor_scalar(out=st[:, 24:26], in0=st[:, 16:18], scalar1=LN_EPS, scalar2=None, op0=ALU.add),
             reads=[stb], writes=[stb])
        HN("act", lambda e: e.activation(out=st[:, 28:30], in_=st[:, 24:26], func=AF.Sqrt), reads=[stb], writes=[stb])
        HN("dve", lambda e: e.reciprocal(out=st[:, 20:22], in_=st[:, 28:30]), reads=[stb], writes=[stb])
        for h in range(2):
            HN("dve", lambda e, h=h: e.tensor_scalar(out=ycc[:, h, :], in0=ysb[:, h * 128:(h + 1) * 128],
                                                       scalar1=st[:, 8 + h:9 + h], scalar2=st[:, 20 + h:21 + h],
                                                       op0=ALU.subtract, op1=ALU.mult), reads=[ysbb, stb], writes=[ycb])
        HN("dve", lambda e: e.tensor_tensor(out=ygg[:], in0=ycc[:].rearrange("p h d -> p (h d)"), in1=gg[:], op=ALU.mult),
             reads=[ycb, gb], writes=[ygb])
        state[("c", t)] = (ygg, ygb)

    def stage_c(t):
        ygg, ygb = state.pop(("c", t))
        yt, ytbuf = yTt.next()
        pt2, pt2b = psT.next()
        for h in range(2):
            S.op("pe", lambda e, h=h: e.transpose(out=pt2[:, h, :], in_=ygg[:, h * 128:(h + 1) * 128], identity=ident[:]),
                 reads=[ygb, cb], writes=[pt2b])
        S.op("act", lambda e: e.copy(out=yt[:], in_=pt2[:, 0:2, :]), reads=[pt2b], writes=[ytbuf])
        dma(S, "pool", yT_s[0:256, t * 128:(t + 1) * 128].rearrange("(c p) t -> p c t", p=128), yt[:], reads=[ytbuf], writes=[ytb])

    def group_front(g):
        xTt, xTb = xT.next()
        state[("xT", g)] = (xTt, xTb)
        for tt in range(4):
            t = g * 4 + tt
            if t >= ntiles:
                break
            xft, xfb = xf.next()
            dma(S, "sp", xft[:], x[t * 128:(t + 1) * 128, :], writes=[xfb])
            xbt, xbb = xb.next()
            S.op("act", lambda e, xbt=xbt, xft=xft: e.copy(out=xbt[:], in_=xft[:]), reads=[xfb], writes=[xbb])
            transpose_tile(p, xbt, xbb, ident, cb, psT, xTt[:, :, tt * 128:(tt + 1) * 128], xTb)

    def group_back(g):
        xTt, xTb = state[("xT", g)]
        ncol = min(512, (ntiles - g * 4) * 128)
        for fc in range(4):
            ps, psb = psF.next()
            for k in range(8):
                S.op("pe", lambda e, k=k, ps=ps, fc=fc: e.matmul(ps[:, 0:ncol], lhsT=Win[:, k, 1024 + fc * 128:1024 + (fc + 1) * 128],
                                                               rhs=xTt[:, k, 0:ncol], start=(k == 0), stop=(k == 7)),
                     reads=[xTb, winb[k]], writes=[psb])
            q, qb_ = qke.next()
            S.op("act", lambda e, ps=ps, q=q: e.copy(out=q[:, 0:ncol], in_=ps[:, 0:ncol]), reads=[psb], writes=[qb_])
            dma(S, "pool", qkdT_s[fc * 128:(fc + 1) * 128, g * 512:g * 512 + ncol], q[:, 0:ncol], reads=[qb_], writes=[qkb])

    order = []
    group_front(0)
    for g in range(ngroups):
        for tt in range(4):
            t = g * 4 + tt
            if t >= ntiles:
                break
            stage_a(t)
            if t >= 1:
                stage_b(t - 1)
            if t >= 2:
                stage_c(t - 2)
            if tt == 1 and g + 1 < ngroups:
                group_front(g + 1)
        group_back(g)
    stage_b(ntiles - 1)
    if ntiles >= 2:
        stage_c(ntiles - 2)
    stage_c(ntiles - 1)
    p.finish()


DSA_PAT = (1, 4, 16)


def phase_l0b(nc, qkdT_s, vd_s, bm_d, yT_s, heads=range(4)):
    p = P(nc, "l0b")
    S = p.S
    QT = p.rot_sb(2, [64, S_LEN], BF16, "QT")
    KT = p.rot_sb(2, [64, S_LEN], BF16, "KT")
    VA = [p.rot_sb(2, [128, 32, 128], BF16, f"VA{i}") for i in range(3)]
    BM = p.rot_sb(2, [128, 3, 256], F32, "BM")
    acc = p.rot_sb(2, [128, S_LEN], F32, "acc")
    rd = p.rot_sb(1, [64, S_LEN], F32, "rd")
    yd = p.rot_sb(2, [64, S_LEN], BF16, "yd")
    Lg = p.rot_sb(4, [128, 256], F32, "Lg")
    Pe = p.rot_sb(5, [128, 256], BF16, "Pe")
    psS = p.rot_ps(4, [128, 256], F32, "psS")
    psO = p.rot_ps(3, [128, 128], F32, "psO")
    ydb = Buf("yd_s")
    vdt = vd_s.tensor
    vab = {}
    for i, r in enumerate(DSA_PAT):
        for j in range(2):
            vab[(i, j)] = bufs(f"va{i}{j}", r)
            t_ = VA[i].t[j]
            S.op("pool", lambda e, t_=t_: e.memset(t_[:, :, 64:128], 1.0), writes=vab[(i, j)])
    def do_head(h):
        qt, qtb = QT.next()
        kt, ktb = KT.next()
        dma(S, "sp", qt[:], qkdT_s[h * 64:(h + 1) * 64, :], writes=[qtb])
        dma(S, "sp", kt[:], qkdT_s[256 + h * 64:256 + (h + 1) * 64, :], writes=[ktb])
        bm, bmb = BM.next()
        dma(S, "sp", bm[:], bm_d[h], writes=[bmb])
        va = []
        for i, r in enumerate(DSA_PAT):
            slot = VA[i].i % 2
            t_, _unused = VA[i].next()
            bl = vab[(i, slot)]
            nb = 32 // r
            for res in range(r):
                src = bass.AP(vdt, (res * 256) + h * 64, [[r * 256, 128], [128 * r * 256, nb], [1, 64]])
                dma(S, "sp", t_[:, res * nb:(res + 1) * nb, 0:64], src, writes=[bl[res]])
            va.append((t_, bl))
        ac, acb = acc.next()
        blocks = []
        for i, r in enumerate(DSA_PAT):
            nb = 32 // r
            for res in range(r):
                for n in range(nb):
                    blocks.append((i, r, nb, res, n))

        def stage1(i, r, nb, res, n):
            c0 = res + 128 * r * n
            qs = slice(c0, c0 + 127 * r + 1, r)
            ps, psb = psS.next()
            S.op("pe", lambda e: e.matmul(ps[:, 0:128], lhsT=kt[:, qs], rhs=qt[:, qs], start=True, stop=True),
                 reads=[ktb, qtb], writes=[psb])
            w = 128
            if n > 0:
                c1 = res + 128 * r * (n - 1)
                ks = slice(c1, c1 + 127 * r + 1, r)
                S.op("pe", lambda e: e.matmul(ps[:, 128:256], lhsT=kt[:, ks], rhs=qt[:, qs], start=True, stop=True),
                     reads=[ktb, qtb], writes=[psb])
                w = 256
            lg, lgb = Lg.next()
            S.op("dve", lambda e: e.scalar_tensor_tensor(out=lg[:, 0:w], in0=ps[:, 0:w], scalar=0.125,
                                                         in1=bm[:, i, 0:w], op0=ALU.mult, op1=ALU.add),
                 reads=[psb, bmb], writes=[lgb])
            pe_, peb = Pe.next()
            S.op("act", lambda e: e.activation(out=pe_[:, 0:w], in_=lg[:, 0:w], func=AF.Exp),
                 reads=[lgb], writes=[peb])
            return (pe_, peb, qs)

        def stage2(i, r, nb, res, n, pe_, peb, qs):
            vt, vbl = va[i]
            vtb = vbl[res]
            po, pob = psO.next()
            ti = res * nb + n
            S.op("pe", lambda e: e.matmul(po[:], lhsT=vt[:, ti, :], rhs=pe_[:, 0:128], start=True, stop=(n == 0)),
                 reads=[vtb, peb], writes=[pob])
            if n > 0:
                S.op("pe", lambda e: e.matmul(po[:], lhsT=vt[:, ti - 1, :], rhs=pe_[:, 128:256], start=False, stop=True),
                     reads=[vtb, peb], writes=[pob])
            if i == 0:
                S.op("act", lambda e: e.copy(out=ac[:, qs], in_=po[:]), reads=[pob], writes=[acb])
            else:
                S.op("dve", lambda e: e.tensor_tensor(out=ac[:, qs], in0=ac[:, qs], in1=po[:], op=ALU.add),
                     reads=[pob, acb], writes=[acb])

        LAG = 2
        pend = []
        for bi, blk in enumerate(blocks):
            pend.append(stage1(*blk))
            if bi >= LAG:
                stage2(*blocks[bi - LAG], *pend[bi - LAG])
        for bi in range(max(0, len(blocks) - LAG), len(blocks)):
            stage2(*blocks[bi], *pend[bi])
        rdt, rdb = rd.next()
        ydt, ydtb = yd.next()
        S.op("dve", lambda e, ac=ac, rdt=rdt: e.reciprocal(out=rdt[:], in_=ac[64:128, :]), reads=[acb], writes=[rdb])
        S.op("dve", lambda e, ac=ac, rdt=rdt, ydt=ydt: e.tensor_tensor(out=ydt[:], in0=ac[0:64, :], in1=rdt[:], op=ALU.mult),
             reads=[acb, rdb], writes=[ydtb])
        dma(S, "pool", yT_s[256 + h * 64:256 + (h + 1) * 64, :], ydt[:], reads=[ydtb], writes=[ydb])

    for h in heads:
        do_head(h)
    p.finish()


def bcast_rows(ap_row, n=128):
    return bass.AP(ap_row.tensor, ap_row.offset, [[0, n]] + [list(a) for a in ap_row.ap])


def emit_ln_and_store(p, z, zb, gB, bB, cbuf, xo_rot, xob_rot, sqr, stt, ident, psT, xoT, xoTb, tt, x_out, t):
    S = p.S
    xo, xob = xo_rot.next() if xo_rot is not None else (z, zb)
    sq, sqb = sqr.next()
    st, stb = stt.next()
    layer_norm_tile(p, z, zb, gB, bB, cbuf, xo, xob, sq, sqb, st, stb)
    dma(S, "pool", x_out[t * 128:(t + 1) * 128, :], xo[:], reads=[xob], writes=[Buf()])
    if xoT is not None:
        xbt, xbb = xob_rot.next()
        S.op("act", lambda e: e.copy(out=xbt[:], in_=xo[:]), reads=[xob], writes=[xbb])
        transpose_tile(p, xbt, xbb, ident, cbuf, psT, xoT[:, :, tt * 128:(tt + 1) * 128], xoTb, evac="act")


def phase_outproj(nc, name, x_in, yT_s, sel_d, w_out, ln_g, ln_b, ident_d, x_out, xT_out, ntiles=16, halo_x=None, xhT_out=None):
    p = P(nc, name)
    S = p.S
    Wout = p.sb([128, 8, 1024], BF16, "wout")
    wb = bufs("w", 8)
    stage = p.rot_sb(4, [128, 1024], F32, "stage")
    ident = p.sb([128, 128], BF16, "ident")
    identf = p.sb([128, 128], F32, "identf")
    gB = p.sb([128, 1024], F32, "gB")
    bB = p.sb([128, 1024], F32, "bB")
    cb = Buf("const")
    sel = p.sb([128, 2], F32, "sel")
    dma(S, "sp", sel[:], sel_d[:, :], writes=[cb])
    dma(S, "sp", identf[:], ident_d[:, :], writes=[cb])
    dma(S, "sp", gB[:], bcast_rows(ln_g), writes=[cb])
    dma(S, "sp", bB[:], bcast_rows(ln_b), writes=[cb])
    S.op("dve", lambda e: e.tensor_copy(out=ident[:], in_=identf[:]), reads=[cb], writes=[cb])
    load_weight(p, Wout, wb, w_out, 8, 1024, stage, cast_engs=("act", "dve"), stage_cols=1024)
    yTg = p.rot_sb(2, [128, 8, 512], BF16, "yTg")
    yTa = p.rot_sb(2, [128, 8, 512], BF16, "yTa")
    yTb = p.rot_sb(2, [128, 8, 512], BF16, "yTb")
    xf = p.rot_sb(3, [128, 1024], F32, "xf")
    zt = p.rot_sb(5, [128, 1024], F32, "z")
    xo = p.rot_sb(4, [128, 1024], F32, "xo")
    xob = p.rot_sb(4, [128, 1024], BF16, "xob")
    sqr = p.rot_sb(2, [128, 1024], F32, "sq")
    stt = p.rot_sb(2, [128, 8], F32, "st")
    xoT = p.rot_sb(2, [128, 8, 512], BF16, "xoT")
    psF = p.rot_ps(4, [128, 512], F32, "psF")
    psT = p.rot_ps(2, [128, 8, 128], BF16, "psT")
    ngroups = (ntiles + 3) // 4
    pending = []
    cur_pair = []

    def flush_pair():
        if not pending:
            return
        tl = []
        for (z, zb, xoTt, xoTb, tt, t, g, ncol, last) in pending:
            xo_, xob_ = xo.next()
            sq_, sqb_ = sqr.next()
            st_, stb_ = stt.next()
            tl.append((z, zb, xo_, xob_, sq_, sqb_, st_, stb_))
        layer_norm_multi(p, tl, gB, bB, cb)
        for (z, zb, xoTt, xoTb, tt, t, g, ncol, last), (_, _, xo_, xob_, _, _, _, _) in zip(pending, tl):
            dma(S, "pool", x_out[t * 128:(t + 1) * 128, :], xo_[:], reads=[xob_], writes=[Buf()])
            xbt, xbb = xob.next()
            S.op("act", lambda e, xbt=xbt, xo_=xo_: e.copy(out=xbt[:], in_=xo_[:]), reads=[xob_], writes=[xbb])
            transpose_tile(p, xbt, xbb, ident, cb, psT, xoTt[:, :, tt * 128:(tt + 1) * 128], xoTb, evac="act")
            if last:
                dma(S, "pool", xT_out[:, g * 512:g * 512 + ncol].rearrange("(c p) t -> p c t", p=128), xoTt[:, :, 0:ncol],
                    reads=[xoTb], writes=[Buf()])
        pending.clear()

    xhT_t = p.sb([128, 8, 128], BF16, "xhT") if halo_x is not None else None

    def do_halo():
        HC = S_LEN // 2 - 128
        yh, yhb = yTa.next()
        dma(S, "sp", yh[:, :, 0:128], yT_s[:, HC:HC + 128].rearrange("(c p) t -> p c t", p=128), writes=[yhb])
        xft, xfb = xf.next()
        dma(S, "sp", xft[:], halo_x, writes=[xfb])
        z, zb = zt.next()
        for nb in range(2):
            ps, psb = psF.next()
            for c in range(8):
                S.op("pe", lambda e, ps=ps, c=c, nb=nb: e.matmul(ps[:], lhsT=yh[:, c, 0:128], rhs=Wout[:, c, nb * 512:(nb + 1) * 512],
                                                              start=(c == 0), stop=(c == 7)), reads=[yhb, wb[c]], writes=[psb])
            S.op("dve", lambda e, ps=ps, nb=nb: e.scalar_tensor_tensor(out=z[:, nb * 512:(nb + 1) * 512], in0=xft[:, nb * 512:(nb + 1) * 512],
                                                                      scalar=ALPHA, in1=ps[:], op0=ALU.mult, op1=ALU.add),
                 reads=[psb, xfb], writes=[zb])
        xo_, xob_ = xo.next()
        sq_, sqb_ = sqr.next()
        st_, stb_ = stt.next()
        layer_norm_tile(p, z, zb, gB, bB, cb, xo_, xob_, sq_, sqb_, st_, stb_)
        xbt, xbb = xob.next()
        S.op("act", lambda e: e.copy(out=xbt[:], in_=xo_[:]), reads=[xob_], writes=[xbb])
        xhT, xhTb = xhT_t, Buf("xhT")
        transpose_tile(p, xbt, xbb, ident, cb, psT, xhT[:, :, 0:128], xhTb, evac="dve")
        dma(S, "pool", xhT_out[:, :].rearrange("(c p) t -> p c t", p=128), xhT[:, :, 0:128], reads=[xhTb], writes=[Buf()])

    for g in range(ngroups):
        ncol = min(512, (ntiles - g * 4) * 128)
        if g == 2 and halo_x is not None:
            do_halo()
        yt, ytb = yTg.next()
        ya, yab = yTa.next()
        yb_, ybb = yTb.next()
        HALF = S_LEN // 2
        dma(S, "sp", ya[:], yT_s[:, g * 512:(g + 1) * 512].rearrange("(c p) t -> p c t", p=128), writes=[yab])
        dma(S, "sp", yb_[:], yT_s[:, HALF + g * 512:HALF + (g + 1) * 512].rearrange("(c p) t -> p c t", p=128), writes=[ybb])
        S.op("act", lambda e, yb_=yb_: e.activation(out=yb_[:], in_=yb_[:], func=AF.Identity, scale=sel[:, 1:2]),
             reads=[ybb, cb], writes=[ybb])
        S.op("dve", lambda e, ya=ya, yb_=yb_, yt=yt: e.scalar_tensor_tensor(out=yt[:], in0=ya[:], scalar=sel[:, 0:1], in1=yb_[:], op0=ALU.mult, op1=ALU.add),
             reads=[yab, ybb, cb], writes=[ytb])
        xoTt, xoTb = xoT.next()
        for tt in range(ncol // 128):
            t = g * 4 + tt
            xft, xfb = xf.next()
            dma(S, "sp", xft[:], x_in[t * 128:(t + 1) * 128, :], writes=[xfb])
            z, zb = zt.next()
            for nb in range(2):
                ps, psb = psF.next()
                for c in range(8):
                    S.op("pe", lambda e, ps=ps, c=c, nb=nb, tt=tt, yt=yt: e.matmul(ps[:], lhsT=yt[:, c, tt * 128:(tt + 1) * 128],
                                                                                  rhs=Wout[:, c, nb * 512:(nb + 1) * 512],
                                                                                  start=(c == 0), stop=(c == 7)),
                         reads=[ytb, wb[c]], writes=[psb])
                S.op("dve", lambda e, ps=ps, nb=nb, z=z, xft=xft: e.scalar_tensor_tensor(out=z[:, nb * 512:(nb + 1) * 512],
                                                                                      in0=xft[:, nb * 512:(nb + 1) * 512], scalar=ALPHA,
                                                                                      in1=ps[:], op0=ALU.mult, op1=ALU.add),
                     reads=[psb, xfb], writes=[zb])
            cur_pair.append((z, zb, xoTt, xoTb, tt, t, g, ncol, tt == ncol // 128 - 1))
            if len(cur_pair) == 2:
                flush_pair()
                pending.extend(cur_pair)
                cur_pair.clear()
    flush_pair()
    pending.extend(cur_pair)
    cur_pair.clear()
    flush_pair()
    p.finish()


def phase_ffn(nc, name, x_in, xT_in, w_up, cw_d, cb_d, w_dn, ln_g, ln_b, ident_d, x_out, xT_out, ntiles=NT, xhT_in=None, sel_d=None):
    p = P(nc, name)
    S = p.S
    Wup = p.sb([128, 8, 2 * D_FF], BF16, "wup")
    wub = bufs("wu", 8)
    Wdn = p.sb([128, NFC, 1024], BF16, "wdn")
    wdb = bufs("wd", NFC)
    stage = p.rot_sb(4, [128, 352], F32, "stage")
    ident = p.sb([128, 128], BF16, "ident")
    identf = p.sb([128, 128], F32, "identf")
    gB = p.sb([128, 1024], F32, "gB")
    bB = p.sb([128, 1024], F32, "bB")
    cw = p.sb([128, NFC, 3], F32, "cw")
    cbias = p.sb([128, NFC], F32, "cbias")
    hl = p.sb([128, NFC, 2], F32, "halo")
    hlb = bufs("hl", NFC)
    cb = Buf("const")
    dma(S, "sp", identf[:], ident_d[:, :], writes=[cb])
    dma(S, "sp", gB[:], bcast_rows(ln_g), writes=[cb])
    dma(S, "sp", bB[:], bcast_rows(ln_b), writes=[cb])
    dma(S, "sp", cw[:], cw_d[:, :, :], writes=[cb])
    dma(S, "sp", cbias[:], cb_d[:, :], writes=[cb])
    S.op("dve", lambda e: e.tensor_copy(out=ident[:], in_=identf[:]), reads=[cb], writes=[cb])
    S.op("pool", lambda e: e.memset(hl[:], 0.0), writes=hlb)
    xTg = p.rot_sb(1, [128, 8, 512], BF16, "xTg")
    xt0, xtb0 = xTg.next()
    nc0 = min(512, ntiles * 128)
    dma(S, "sp", xt0[:, :, 0:nc0], xT_in[:, 0:nc0].rearrange("(c p) t -> p c t", p=128), writes=[xtb0])
    if xhT_in is not None:
        xh = p.sb([128, 8, 2], BF16, "xh")
        selt = p.sb([128, 2], F32, "sel")
        xhb = Buf("xh")
        dma(S, "sp", xh[:], xhT_in[:, 126:128].rearrange("(c p) t -> p c t", p=128), writes=[xhb])
        dma(S, "sp", selt[:], sel_d[:, :], writes=[xhb])
    NBLK = 2
    FPB = NFC // NBLK
    wubb = [bufs(f"wu{b_}", 8) for b_ in range(NBLK)]
    ci = 0
    for blk in range(NBLK):
        for half in range(2):
            for k in range(8):
                for cc in range(0, FPB * 128, 352):
                    c0 = half * D_FF + blk * FPB * 128 + cc
                    st_, stb_ = stage.next()
                    dma(S, "sp", st_[:, 0:352], w_up[k * 128:(k + 1) * 128, c0:c0 + 352], writes=[stb_])
                    if ci % 2 == 0:
                        S.op("act", lambda e, st_=st_, k=k, c0=c0: e.copy(out=Wup[:, k, c0:c0 + 352], in_=st_[:, 0:352]), reads=[stb_], writes=[wubb[blk][k]])
                    else:
                        S.op("dve", lambda e, st_=st_, k=k, c0=c0: e.tensor_copy(out=Wup[:, k, c0:c0 + 352], in_=st_[:, 0:352]), reads=[stb_], writes=[wubb[blk][k]])
                    ci += 1
    gs = p.rot_sb(2, [128, 514], F32, "gs")
    t1r = p.rot_sb(1, [128, 512], F32, "t1")
    hT = p.sb([128, NFC, 512], BF16, "hT")
    hTb = bufs("hT", NFC)
    xf = p.rot_sb(1, [128, 1024], F32, "xf")
    zt = p.rot_sb(2, [128, 1024], F32, "z")
    xo = None
    xob = p.rot_sb(2, [128, 1024], BF16, "xob")
    sqr = p.rot_sb(1, [128, 1024], F32, "sq")
    stt = p.rot_sb(2, [128, 8], F32, "st")
    xoT = p.rot_sb(2, [128, 8, 128], BF16, "xoT")
    psF = p.rot_ps(6, [128, 512], F32, "psF")
    psT = p.rot_ps(2, [128, 8, 128], BF16, "psT")
    ngroups = (ntiles + 3) // 4
    assert xT_out is not None
    pendT = []

    def ln_pair(pr):
        sq, sqb = sqr.next()
        tl = []
        for (z, zb, t) in pr:
            st, stb = stt.next()
            tl.append((z, zb, st, stb))
        layer_norm_pair_inplace(p, tl, sq, sqb, gB, bB, cb)
        for (z, zb, t) in pr:
            dma(S, "pool", x_out[t * 128:(t + 1) * 128, :], z[:], reads=[zb], writes=[Buf()])
            xbt, xbb = xob.next()
            S.op("act", lambda e, xbt=xbt, z=z: e.copy(out=xbt[:], in_=z[:]), reads=[zb], writes=[xbb])
            pendT.append((xbt, xbb, t))

    def flush_T():
        for (xbt, xbb, t) in pendT:
            xoTt, xoTb = xoT.next()
            transpose_tile(p, xbt, xbb, ident, cb, psT, xoTt[:, :, :], xoTb, evac="act")
            dma(S, "pool", xT_out[:, t * 128:(t + 1) * 128].rearrange("(c p) t -> p c t", p=128), xoTt[:, :, :], reads=[xoTb], writes=[Buf()])
        pendT.clear()

    for g in range(ngroups):
        ncol = min(512, (ntiles - g * 4) * 128)
        if g == 0:
            xt, xtb = xt0, xtb0
        else:
            xt, xtb = xTg.next()
            dma(S, "sp", xt[:, :, 0:ncol], xT_in[:, g * 512:g * 512 + ncol].rearrange("(c p) t -> p c t", p=128), writes=[xtb])
        for fc in range(NFC):
            if g == 0 and xhT_in is not None:
                psh, pshb = psF.next()
                for k in range(8):
                    S.op("pe", lambda e, psh=psh, k=k, fc=fc: e.matmul(psh[:, 0:2], lhsT=Wup[:, k, fc * 128:(fc + 1) * 128], rhs=xh[:, k, :],
                                                                      start=(k == 0), stop=(k == 7)), reads=[xhb, wubb[fc // FPB][k]], writes=[pshb])
                S.op("dve", lambda e, psh=psh, fc=fc: e.tensor_scalar(out=hl[:, fc, :], in0=psh[:, 0:2], scalar1=selt[:, 1:2], scalar2=None, op0=ALU.mult),
                     reads=[pshb, xhb, hlb[fc]], writes=[hlb[fc]])
            psg, psgb = psF.next()
            for k in range(8):
                S.op("pe", lambda e, psg=psg, k=k, fc=fc, xt=xt: e.matmul(psg[:, 0:ncol], lhsT=Wup[:, k, fc * 128:(fc + 1) * 128],
                                                                          rhs=xt[:, k, 0:ncol], start=(k == 0), stop=(k == 7)),
                     reads=[xtb, wubb[fc // FPB][k]], writes=[psgb])
            psu, psub = psF.next()
            for k in range(8):
                S.op("pe", lambda e, psu=psu, k=k, fc=fc, xt=xt: e.matmul(psu[:, 0:ncol], lhsT=Wup[:, k, D_FF + fc * 128:D_FF + (fc + 1) * 128],
                                                                          rhs=xt[:, k, 0:ncol], start=(k == 0), stop=(k == 7)),
                     reads=[xtb, wubb[fc // FPB][k]], writes=[psub])
            gt, gtb = gs.next()
            S.op("act", lambda e, gt=gt, fc=fc: e.copy(out=gt[:, 0:2], in_=hl[:, fc, :]), reads=[hlb[fc]], writes=[gtb])
            S.op("act", lambda e, gt=gt, psg=psg: e.copy(out=gt[:, 2:2 + ncol], in_=psg[:, 0:ncol]), reads=[psgb], writes=[gtb])
            S.op("act", lambda e, gt=gt, fc=fc: e.copy(out=hl[:, fc, :], in_=gt[:, ncol:ncol + 2]), reads=[gtb], writes=[hlb[fc]])
            t1, t1b = t1r.next()
            S.op("dve", lambda e, gt=gt, t1=t1, fc=fc: e.tensor_scalar(out=t1[:, 0:ncol], in0=gt[:, 2:2 + ncol], scalar1=cw[:, fc, 2:3],
                                                                     scalar2=None, op0=ALU.mult), reads=[gtb, cb], writes=[t1b])
            S.op("dve", lambda e, gt=gt, t1=t1, fc=fc: e.scalar_tensor_tensor(out=t1[:, 0:ncol], in0=gt[:, 1:1 + ncol], scalar=cw[:, fc, 1:2],
                                                                            in1=t1[:, 0:ncol], op0=ALU.mult, op1=ALU.add),
                 reads=[gtb, cb, t1b], writes=[t1b])
            S.op("dve", lambda e, gt=gt, t1=t1, fc=fc: e.scalar_tensor_tensor(out=t1[:, 0:ncol], in0=gt[:, 0:ncol], scalar=cw[:, fc, 0:1],
                                                                            in1=t1[:, 0:ncol], op0=ALU.mult, op1=ALU.add),
                 reads=[gtb, cb, t1b], writes=[t1b])
            a, ab = t1, t1b
            S.op("act", lambda e, a=a, t1=t1, fc=fc: e.activation(out=a[:, 0:ncol], in_=t1[:, 0:ncol], func=AF.Silu, bias=cbias[:, fc:fc + 1]),
                 reads=[t1b, cb], writes=[ab])
            S.op("dve", lambda e, a=a, psu=psu, fc=fc: e.tensor_tensor(out=hT[:, fc, 0:ncol], in0=a[:, 0:ncol], in1=psu[:, 0:ncol], op=ALU.mult),
                 reads=[ab, psub], writes=[hTb[fc]])
            if g == 0:
                for c0 in range(0, 1024, 352):
                    cw_ = min(352, 1024 - c0)
                    st_, stb_ = stage.next()
                    dma(S, "sp", st_[:, 0:cw_], w_dn[fc * 128:(fc + 1) * 128, c0:c0 + cw_], writes=[stb_])
                    if (fc + c0 // 352) % 2 == 0:
                        S.op("act", lambda e, st_=st_, fc=fc, c0=c0, cw_=cw_: e.copy(out=Wdn[:, fc, c0:c0 + cw_], in_=st_[:, 0:cw_]),
                             reads=[stb_], writes=[wdb[fc]])
                    else:
                        S.op("dve", lambda e, st_=st_, fc=fc, c0=c0, cw_=cw_: e.tensor_copy(out=Wdn[:, fc, c0:c0 + cw_], in_=st_[:, 0:cw_]),
                             reads=[stb_], writes=[wdb[fc]])
        flush_T()
        pair = []
        for tt in range(ncol // 128):
            t = g * 4 + tt
            xft, xfb = xf.next()
            dma(S, "sp", xft[:], x_in[t * 128:(t + 1) * 128, :], writes=[xfb])
            z, zb = zt.next()
            for nb in range(2):
                ps, psb = psF.next()
                for fc in range(NFC):
                    S.op("pe", lambda e, ps=ps, fc=fc, nb=nb, tt=tt: e.matmul(ps[:], lhsT=hT[:, fc, tt * 128:(tt + 1) * 128],
                                                                             rhs=Wdn[:, fc, nb * 512:(nb + 1) * 512],
                                                                             start=(fc == 0), stop=(fc == NFC - 1)),
                         reads=[hTb[fc], wdb[fc]], writes=[psb])
                S.op("dve", lambda e, ps=ps, nb=nb, z=z, xft=xft: e.scalar_tensor_tensor(out=z[:, nb * 512:(nb + 1) * 512],
                                                                                      in0=xft[:, nb * 512:(nb + 1) * 512], scalar=ALPHA,
                                                                                      in1=ps[:], op0=ALU.mult, op1=ALU.add),
                     reads=[psb, xfb], writes=[zb])
            pair.append((z, zb, t))
            if len(pair) == 2:
                if tt == 3:
                    flush_T()
                ln_pair(pair)
                pair = []
        if pair:
            ln_pair(pair)
    flush_T()
    p.finish()


def phase_l1a(nc, xT_in, w_in, cw_d, qkT_s, v_s, o_s, gi_s, gf_s):
    p = P(nc, "l1a")
    S = p.S
    Win = p.sb([128, 8, 2052], BF16, "win")
    wb = bufs("w", 8)
    stage = p.rot_sb(4, [128, 2052], F32, "stage")
    cw = p.sb([128, 8, 4], F32, "cw")
    hl = p.sb([128, 8, 3], F32, "halo")
    hlb = bufs("hl", 8)
    cb = Buf("const")
    dma(S, "sp", cw[:], cw_d[:, :, :], writes=[cb])
    S.op("pool", lambda e: e.memset(hl[:], 0.0), writes=hlb)
    load_weight(p, Win, wb, w_in, 8, 2052, stage, cast_engs=("act", "dve"), stage_cols=2052)
    xTg = p.rot_sb(2, [128, 8, 512], BF16, "xTg")
    gs = p.rot_sb(2, [128, 515], F32, "gs")
    t1r = p.rot_sb(2, [128, 512], F32, "t1")
    qke = p.rot_sb(2, [128, 512], BF16, "qke")
    vt = p.rot_sb(2, [128, 512], BF16, "vt")
    ot = p.rot_sb(2, [128, 512], F32, "ot")
    gr = p.rot_sb(2, [2, 512], F32, "gr")
    psF = p.rot_ps(7, [128, 512], F32, "psF")
    for g in range(NG):
        xt, xtb = xTg.next()
        for j in range(4):
            r0 = j * 512 + (g // 4) * 256
            dma(S, "sp", xt[:, 2 * j:2 * j + 2, :], xT_in[r0:r0 + 256, (g % 4) * 512:(g % 4 + 1) * 512].rearrange("(c p) t -> p c t", p=128), writes=[xtb])
        for fc in range(8):
            ps, psb = psF.next()
            for k in range(8):
                S.op("pe", lambda e, ps=ps, k=k, fc=fc, xt=xt: e.matmul(ps[:], lhsT=Win[:, k, fc * 128:(fc + 1) * 128], rhs=xt[:, k, :],
                                                                        start=(k == 0), stop=(k == 7)), reads=[xtb, wb[k]], writes=[psb])
            gt, gtb = gs.next()
            S.op("act", lambda e, gt=gt, fc=fc: e.copy(out=gt[:, 0:3], in_=hl[:, fc, :]), reads=[hlb[fc]], writes=[gtb])
            S.op("act", lambda e, gt=gt, ps=ps: e.copy(out=gt[:, 3:515], in_=ps[:]), reads=[psb], writes=[gtb])
            S.op("act", lambda e, gt=gt, fc=fc: e.copy(out=hl[:, fc, :], in_=gt[:, 512:515]), reads=[gtb], writes=[hlb[fc]])
            t1, t1b = t1r.next()
            S.op("dve", lambda e, gt=gt, t1=t1, fc=fc: e.tensor_scalar(out=t1[:], in0=gt[:, 3:515], scalar1=cw[:, fc, 3:4], scalar2=None, op0=ALU.mult),
                 reads=[gtb, cb], writes=[t1b])
            for j in (2, 1, 0):
                S.op("dve", lambda e, gt=gt, t1=t1, fc=fc, j=j: e.scalar_tensor_tensor(out=t1[:], in0=gt[:, j:j + 512], scalar=cw[:, fc, j:j + 1],
                                                                                     in1=t1[:], op0=ALU.mult, op1=ALU.add),
                     reads=[gtb, cb, t1b], writes=[t1b])
            q, qb_ = qke.next()
            S.op("act", lambda e, q=q, t1=t1: e.activation(out=q[:], in_=t1[:], func=AF.Silu), reads=[t1b], writes=[qb_])
            dma(S, "pool", qkT_s[fc * 128:(fc + 1) * 128, g * 512:(g + 1) * 512], q[:], reads=[qb_], writes=[Buf()])
        for (c0, dst) in ((2048, gi_s), (2050, gf_s)):
            ps, psb = psF.next()
            for k in range(8):
                S.op("pe", lambda e, ps=ps, k=k, c0=c0, xt=xt: e.matmul(ps[0:2, :], lhsT=Win[:, k, c0:c0 + 2], rhs=xt[:, k, :],
                                                                        start=(k == 0), stop=(k == 7)), reads=[xtb, wb[k]], writes=[psb])
            r_, rb = gr.next()
            S.op("act", lambda e, ps=ps, r_=r_: e.copy(out=r_[:], in_=ps[0:2, :]), reads=[psb], writes=[rb])
            dma(S, "pool", dst[:, g * 512:(g + 1) * 512], r_[:], reads=[rb], writes=[Buf()])
        for tt in range(4):
            t = g * 4 + tt
            v, vb = vt.next()
            o, ob = ot.next()
            ps, psb = psF.next()
            for k in range(8):
                S.op("pe", lambda e, ps=ps, k=k, tt=tt, xt=xt: e.matmul(ps[:], lhsT=xt[:, k, tt * 128:(tt + 1) * 128], rhs=Win[:, k, 1024:1536],
                                                                       start=(k == 0), stop=(k == 7)), reads=[xtb, wb[k]], writes=[psb])
            S.op("act", lambda e, ps=ps, v=v: e.copy(out=v[:], in_=ps[:]), reads=[psb], writes=[vb])
            ps, psb = psF.next()
            for k in range(8):
                S.op("pe", lambda e, ps=ps, k=k, tt=tt, xt=xt: e.matmul(ps[:], lhsT=xt[:, k, tt * 128:(tt + 1) * 128], rhs=Win[:, k, 1536:2048],
                                                                       start=(k == 0), stop=(k == 7)), reads=[xtb, wb[k]], writes=[psb])
            S.op("act", lambda e, ps=ps, o=o: e.activation(out=o[:], in_=ps[:], func=AF.Sigmoid), reads=[psb], writes=[ob])
            dma(S, "pool", v_s[t * 128:(t + 1) * 128, :], v[:], reads=[vb], writes=[Buf()])
            dma(S, "pool", o_s[t * 128:(t + 1) * 128, :], o[:], reads=[ob], writes=[Buf()])
    p.finish()


def phase_l1p(nc, gi_s, gf_s, gb_d, ident_d, e127_d, cols_s):
    p = P(nc, "l1p")
    S = p.S
    N = S_LEN
    gi = p.sb([2, N], F32, "gi")
    gf = p.sb([2, N], F32, "gf")
    A = p.sb([2, N], F32, "A")
    Bt = p.sb([2, N], F32, "B")
    U = p.sb([2, N], F32, "U")
    Mx = p.sb([2, N], F32, "Mx")
    Mg = p.sb([2, 32], F32, "Mg")
    R = [p.sb([2, N], F32, f"R{i}") for i in range(3)]
    gb = p.sb([2, 2], F32, "gb")
    identf = p.sb([128, 128], F32, "identf")
    e127 = p.sb([128, 128], F32, "e127")
    colsb = p.sb([128, 4, 64], F32, "colsb")
    ps3 = [p.ps([128, 32, 2], F32, f"pc{i}") for i in range(3)]
    psg = p.ps([128, 64], F32, "pg")
    b = Buf("all")

    def op(eng, fn):
        S.op(eng, fn, reads=[b], writes=[b])

    dma(S, "sp", gi[:], gi_s[:, :], writes=[b])
    dma(S, "sp", gf[:], gf_s[:, :], writes=[b])
    dma(S, "sp", gb[:], gb_d[:, :], writes=[b])
    dma(S, "sp", identf[:], ident_d[:, :], writes=[b])
    dma(S, "sp", e127[:], e127_d[:, :], writes=[b])
    op("dve", lambda e: e.tensor_scalar(out=gf[:], in0=gf[:], scalar1=gb[:, 1:2], scalar2=None, op0=ALU.add))
    op("act", lambda e: e.activation(out=gf[:], in_=gf[:], func=AF.Exp, scale=-1.0))
    op("dve", lambda e: e.tensor_scalar(out=gf[:], in0=gf[:], scalar1=1.0, scalar2=None, op0=ALU.add))
    op("act", lambda e: e.activation(out=A[:], in_=gf[:], func=AF.Ln))

    def scan(src, tmp, alu):
        cur, oth = src, tmp
        sft = 1
        while sft < N:
            op("act", lambda e, cur=cur, oth=oth, sft=sft: e.copy(out=oth[:, 0:sft], in_=cur[:, 0:sft]))
            op("dve", lambda e, cur=cur, oth=oth, sft=sft: e.tensor_tensor(out=oth[:, sft:N], in0=cur[:, sft:N], in1=cur[:, 0:N - sft], op=alu))
            cur, oth = oth, cur
            sft *= 2
        return cur

    cs = scan(A, Bt, ALU.add)
    op("dve", lambda e: e.tensor_scalar(out=gi[:], in0=gi[:], scalar1=gb[:, 0:1], scalar2=None, op0=ALU.add))
    op("dve", lambda e: e.tensor_tensor(out=U[:], in0=gi[:], in1=cs[:], op=ALU.add))
    op("act", lambda e: e.copy(out=Mx[:], in_=U[:]))
    gcm = scan(Mx, Bt, ALU.max)
    op("dve", lambda e: e.tensor_scalar(out=gcm[:], in0=gcm[:], scalar1=0.0, scalar2=None, op0=ALU.max))
    Mx3 = gcm[:].rearrange("p (n c) -> p n c", c=128)
    op("pool", lambda e: e.memset(Mg[:], 0.0))
    op("dve", lambda e: e.tensor_copy(out=Mg[:, 1:32], in_=Mx3[:, 0:31, 127]))
    Mgb = Mg[:].unsqueeze(2).to_broadcast([2, 32, 128])
    op("dve", lambda e: e.tensor_tensor(out=R[0][:].rearrange("p (n c) -> p n c", c=128), in0=U[:].rearrange("p (n c) -> p n c", c=128),
                                        in1=Mgb, op=ALU.subtract))
    op("dve", lambda e: e.tensor_scalar(out=R[0][:], in0=R[0][:], scalar1=-math.log(16.0), scalar2=None, op0=ALU.add))
    op("act", lambda e: e.activation(out=R[0][:], in_=R[0][:], func=AF.Exp))
    op("dve", lambda e: e.tensor_tensor(out=R[1][:].rearrange("p (n c) -> p n c", c=128), in0=Mx3, in1=Mgb, op=ALU.subtract))
    op("act", lambda e: e.activation(out=R[1][:], in_=R[1][:], func=AF.Exp, scale=-1.0))
    op("dve", lambda e: e.tensor_tensor(out=R[2][:], in0=cs[:], in1=gcm[:], op=ALU.subtract))
    op("act", lambda e: e.activation(out=R[2][:], in_=R[2][:], func=AF.Exp))
    for i in range(3):
        for n in range(32):
            op("pe", lambda e, i=i, n=n: e.transpose(out=ps3[i][:, n, :], in_=R[i][:, n * 128:(n + 1) * 128], identity=identf[0:2, 0:2]))
        op("act", lambda e, i=i: e.copy(out=colsb[:, i, :], in_=ps3[i][:].rearrange("p n h -> p (n h)")))
    op("pe", lambda e: e.matmul(psg[:], lhsT=e127[:], rhs=colsb[:, 1, :], start=True, stop=True))
    op("act", lambda e: e.copy(out=colsb[:, 3, :], in_=psg[:]))
    for i in range(4):
        dma(S, "pool", cols_s[i], colsb[:, i, :], reads=[b], writes=[Buf()])
    p.finish()


def phase_l1b(nc, qkT_s, v_s, o_s, cols_s, ident_d, mask_d, yT_s):
    p = P(nc, "l1b")
    S = p.S
    ident = p.sb([128, 128], BF16, "ident")
    identf = p.sb([128, 128], F32, "identf")
    mask = p.sb([128, 128], F32, "mask")
    cols = p.sb([128, 4, 32, 2], F32, "cols")
    cb = Buf("const")
    dma(S, "sp", identf[:], ident_d[:, :], writes=[cb])
    dma(S, "sp", mask[:], mask_d[:, :], writes=[cb])
    for i in range(4):
        dma(S, "sp", cols[:, i, :, :].rearrange("p n h -> p (n h)"), cols_s[i], writes=[cb])
    S.op("dve", lambda e: e.tensor_copy(out=ident[:], in_=identf[:]), reads=[cb], writes=[cb])
    qT = p.rot_sb(2, [128, 4, 128], BF16, "qT")
    kT = p.rot_sb(2, [128, 4, 128], BF16, "kT")
    Va = p.rot_sb(2, [128, 2, 257], BF16, "Va")
    ot = p.rot_sb(2, [128, 512], F32, "ot")
    Ka = p.rot_sb(2, [128, 512], BF16, "Ka")
    PT = p.rot_sb(2, [128, 2, 128], BF16, "PT")
    C32 = p.sb([128, 2, 2, 257], F32, "C32")
    Cbf = p.sb([128, 2, 2, 257], BF16, "Cbf")
    c32b = [[Buf() for _ in range(2)] for _ in range(4)]
    cbfb = [[Buf() for _ in range(2)] for _ in range(4)]
    stt = p.rot_sb(2, [128, 24], F32, "st")
    yb = p.rot_sb(2, [128, 512], BF16, "yb")
    yTt = p.rot_sb(2, [128, 4, 128], BF16, "yTt")
    psT = p.rot_ps(2, [128, 8, 128], BF16, "psT")
    psF = p.rot_ps(6, [128, 512], F32, "psF")
    for j in range(2):
        t_, b_ = Va.t[j], Va.b[j]
        S.op("pool", lambda e, t_=t_: e.memset(t_[:, :, 256:257], 1.0), writes=[b_])
    def do_tile(n):
        cs_ = slice(n * 128, (n + 1) * 128)
        q, qb_ = qT.next()
        k, kb_ = kT.next()
        va, vab = Va.next()
        o, ob = ot.next()
        dma(S, "sp", q[:], qkT_s[0:512, cs_].rearrange("(c p) t -> p c t", p=128), writes=[qb_])
        dma(S, "sp", k[:], qkT_s[512:1024, cs_].rearrange("(c p) t -> p c t", p=128), writes=[kb_])
        dma(S, "sp", va[:, :, 0:256], v_s[cs_, :].rearrange("t (h e) -> t h e", h=2), writes=[vab])
        dma(S, "sp", o[:], o_s[cs_, :], writes=[ob])
        pt, ptb = psT.next()
        for c in range(4):
            S.op("pe", lambda e, c=c, pt=pt, k=k: e.transpose(out=pt[:, c, :], in_=k[:, c, :], identity=ident[:]), reads=[kb_, cb], writes=[ptb])
        ka, kab = Ka.next()
        for h in range(2):
            S.op("dve", lambda e, h=h, pt=pt, ka=ka: e.tensor_scalar(out=ka[:, h * 256:(h + 1) * 256], in0=pt[:, 2 * h:2 * h + 2, :].rearrange("p c d -> p (c d)"),
                                                                   scalar1=cols[:, 0, n, h:h + 1], scalar2=None, op0=ALU.mult),
                 reads=[ptb, cb], writes=[kab])
        pss, pssb = psF.next()
        for h in range(2):
            for dc in range(2):
                S.op("pe", lambda e, h=h, dc=dc, pss=pss, k=k, q=q: e.matmul(pss[:, h * 128:(h + 1) * 128], lhsT=k[:, 2 * h + dc, :], rhs=q[:, 2 * h + dc, :],
                                                                          start=(dc == 0), stop=(dc == 1)), reads=[kb_, qb_], writes=[pssb])
        ptt, pttb = PT.next()
        for h in range(2):
            S.op("dve", lambda e, h=h, pss=pss, ptt=ptt: e.scalar_tensor_tensor(out=ptt[:, h, :], in0=pss[:, h * 128:(h + 1) * 128], scalar=cols[:, 0, n, h:h + 1],
                                                                              in1=mask[:], op0=ALU.mult, op1=ALU.mult), reads=[pssb, cb], writes=[pttb])
        st, stb = stt.next()
        pso = []
        for h in range(2):
            po, pob = psF.next()
            S.op("pe", lambda e, h=h, po=po, ptt=ptt, va=va: e.matmul(po[:, 0:257], lhsT=ptt[:, h, :], rhs=va[:, h, :], start=True, stop=(n == 0)),
                 reads=[pttb, vab], writes=[pob])
            if n > 0:
                for dc in range(2):
                    S.op("pe", lambda e, h=h, dc=dc, po=po, q=q: e.matmul(po[:, 0:257], lhsT=q[:, 2 * h + dc, :], rhs=Cbf[:, h, dc, :], start=False, stop=(dc == 1)),
                         reads=[qb_, cbfb[h][dc]], writes=[pob])
            S.op("dve", lambda e, h=h, po=po, st=st: e.tensor_copy(out=st[:, 16 + h:17 + h], in_=po[:, 256:257]), reads=[pob], writes=[stb])
            pso.append((po, pob))
        S.op("dve", lambda e, st=st: e.tensor_scalar(out=st[:, 20:22], in0=st[:, 16:18], scalar1=-1.0, scalar2=None, op0=ALU.mult), reads=[stb], writes=[stb])
        S.op("dve", lambda e, st=st: e.tensor_tensor(out=st[:, 20:22], in0=st[:, 20:22], in1=st[:, 16:18], op=ALU.max), reads=[stb], writes=[stb])
        S.op("dve", lambda e, st=st: e.tensor_tensor(out=st[:, 0:2], in0=st[:, 20:22], in1=cols[:, 1, n, :], op=ALU.mult), reads=[stb, cb], writes=[stb])
        S.op("dve", lambda e, st=st: e.tensor_tensor(out=st[:, 4:6], in0=st[:, 0:2], in1=cols[:, 2, n, :], op=ALU.max), reads=[stb, cb], writes=[stb])
        S.op("dve", lambda e, st=st: e.reciprocal(out=st[:, 8:10], in_=st[:, 4:6]), reads=[stb], writes=[stb])
        S.op("dve", lambda e, st=st: e.tensor_tensor(out=st[:, 12:14], in0=st[:, 8:10], in1=cols[:, 1, n, :], op=ALU.mult), reads=[stb, cb], writes=[stb])
        y, ybuf = yb.next()
        for h in range(2):
            po, pob = pso[h]
            S.op("dve", lambda e, h=h, po=po, y=y, st=st, o=o: e.scalar_tensor_tensor(out=y[:, h * 256:(h + 1) * 256], in0=po[:, 0:256], scalar=st[:, 12 + h:13 + h],
                                                                                    in1=o[:, h * 256:(h + 1) * 256], op0=ALU.mult, op1=ALU.mult),
                 reads=[pob, stb, ob], writes=[ybuf])
        if n < NT - 1:
            for h in range(2):
                for dc in range(2):
                    pk, pkb = psF.next()
                    S.op("pe", lambda e, h=h, dc=dc, pk=pk, ka=ka, va=va: e.matmul(pk[:, 0:257], lhsT=ka[:, h * 256 + dc * 128:h * 256 + (dc + 1) * 128], rhs=va[:, h, :],
                                                                                  start=True, stop=True), reads=[kab, vab], writes=[pkb])
                    if n == 0:
                        S.op("act", lambda e, h=h, dc=dc, pk=pk: e.copy(out=C32[:, h, dc, :], in_=pk[:, 0:257]), reads=[pkb], writes=[c32b[h][dc]])
                    else:
                        S.op("dve", lambda e, h=h, dc=dc, pk=pk: e.scalar_tensor_tensor(out=C32[:, h, dc, :], in0=C32[:, h, dc, :], scalar=cols[:, 3, n - 1, h:h + 1],
                                                                                      in1=pk[:, 0:257], op0=ALU.mult, op1=ALU.add),
                             reads=[pkb, c32b[h][dc], cb], writes=[c32b[h][dc]])
                    S.op("act", lambda e, h=h, dc=dc: e.activation(out=Cbf[:, h, dc, :], in_=C32[:, h, dc, :], func=AF.Identity, scale=cols[:, 3, n, h:h + 1]),
                         reads=[c32b[h][dc], cb], writes=[cbfb[h][dc]])
        yt, ytb = yTt.next()
        pt2, pt2b = psT.next()
        for c in range(4):
            S.op("pe", lambda e, c=c, pt2=pt2, y=y: e.transpose(out=pt2[:, c, :], in_=y[:, c * 128:(c + 1) * 128], identity=ident[:]), reads=[ybuf, cb], writes=[pt2b])
        S.op("act", lambda e, pt2=pt2, yt=yt: e.copy(out=yt[:], in_=pt2[:, 0:4, :]), reads=[pt2b], writes=[ytb])
        dma(S, "pool", yT_s[:, cs_].rearrange("(c p) t -> p c t", p=128), yt[:], reads=[ytb], writes=[Buf()])

    for n in range(NT):
        do_tile(n)
    p.finish()


def make_consts():
    ident = np.eye(128, dtype=np.float32)
    mask = (np.arange(128)[:, None] <= np.arange(128)[None, :]).astype(np.float32)
    d = 128
    inv = 1.0 / (10000.0 ** (np.arange(0, d, 2, dtype=np.float64) / d))
    ang = np.arange(S_LEN, dtype=np.float64)[:, None] * inv[None, :]
    cos, sin = np.cos(ang), np.sin(ang)
    pos = np.arange(S_LEN) % 128
    rot = np.zeros((4, S_LEN, 4, 64), np.float64)
    for h in range(4):
        lg = np.log1p(-2.0 ** (-5.0 - h))
        a = np.exp(lg * (pos + 1.0))[:, None]
        b = np.exp(-lg * (pos + 1.0))[:, None] * (d ** -0.5)
        rot[0, :, h] = cos * a
        rot[1, :, h] = sin * a
        rot[2, :, h] = cos * b
        rot[3, :, h] = sin * b
    return dict(ident=ident, mask=mask, rot=rot.reshape(4, S_LEN, 256).astype(np.float32))


def t5_bucket_np(dist):
    exact = 16
    n = np.maximum(dist, 0)
    large = exact + (np.log(np.maximum(n, 1).astype(np.float32) / exact) / math.log(2048 / exact) * (32 - exact)).astype(np.int32)
    large = np.minimum(large, 31)
    return np.where(n < exact, n, large)


def make_bm(rel_bias):
    k = np.arange(128)[:, None]
    q = np.arange(128)[None, :]
    out = np.zeros((8, 128, 3, 256), np.float32)
    for i, r in enumerate((1, 4, 16)):
        dcur = q - k
        dprev = q - k + 128
        for dist, c0 in ((dcur, 0), (dprev, 128)):
            valid = (dist >= 0) & (dist <= 128)
            idx = t5_bucket_np(dist * r)
            b = rel_bias[idx]
            b = np.where(valid[:, :, None], b, np.float32(-30000.0))
            out[:, :, i, c0:c0 + 128] = b.transpose(2, 0, 1)
    return out


PAIRS = [[0, 1], [2, 3], [4, 5], [6, 7]]
HALF = S_LEN // 2


def phase_allgather(nc, src, dst, nrows):
    nch = nrows // 256
    sems = []
    for j in range(nch):
        Sched.uid += 1
        sems.append(nc.alloc_semaphore(name=f"cc{Sched.uid}"))
    with nc.Block() as block:
        @block.gpsimd
        def _(g):
            for j in range(nch):
                g.collective_compute("AllGather", ALU.bypass, replica_groups=PAIRS, ins=[src[j * 256:(j + 1) * 256, :]],
                                     outs=[dst[j * 512:(j + 1) * 512, :]]).then_inc(sems[j])
                g.wait_ge(sems[j], 1)
    nc.all_engine_barrier()
    nc.clear_and_free_semaphores(sems)
    nc.all_engine_barrier()


def phase_allgather_small(nc, src, dst):
    Sched.uid += 1
    sem = nc.alloc_semaphore(name=f"cc{Sched.uid}")
    with nc.Block() as block:
        @block.gpsimd
        def _(g):
            g.collective_compute("AllGather", ALU.bypass, replica_groups=PAIRS, ins=[src], outs=[dst]).then_inc(sem)
            g.wait_ge(sem, 1)
    nc.all_engine_barrier()
    nc.clear_and_free_semaphores([sem])
    nc.all_engine_barrier()


def build_program():
    nc = bass.Bass("TRN2", target_bir_lowering=False)

    def din(name, shape, dt=F32):
        return nc.dram_tensor(name, list(shape), dt, kind="ExternalInput").ap()

    def scr(name, shape, dt=F32):
        return nc.dram_tensor(name, list(shape), dt, kind="Internal").ap()

    x = din("x", [S_LEN, D])
    xh = din("xh", [HALF, D])
    w_in0 = din("w_in0", [D, 1792])
    w_out0 = din("w_out0", [D, D])
    rot = din("rot", [4, S_LEN, 128])
    ident = din("ident", [128, 128])
    mask = din("mask", [128, 128])
    e127 = din("e127", [128, 128])
    retg = din("retg", [128, 2])
    sel = din("sel", [128, 2])
    bm = din("bm", [4, 128, 3, 256])
    ln_g = din("ln_g", [2, 2, D])
    ln_b = din("ln_b", [2, 2, D])
    w_up = [din(f"w_up{l}", [D, 2 * D_FF]) for l in range(2)]
    w_dn = [din(f"w_dn{l}", [D_FF, D]) for l in range(2)]
    cw = [din(f"cw{l}", [128, NFC, 3]) for l in range(2)]
    cbv = [din(f"cb{l}", [128, NFC]) for l in range(2)]
    w_in1 = din("w_in1", [D, 2052])
    w_out1 = din("w_out1", [D, D])
    cwm = din("cwm", [128, 8, 4])
    gb = din("gb", [2, 2])
    out = nc.dram_tensor("out", [HALF, D], F32, kind="ExternalOutput").ap()

    yT_loc = scr("yT_loc", [512, S_LEN], BF16)
    yT_g = scr("yT_g", [1024, S_LEN], BF16)
    qkdT_s = scr("qkdT_s", [512, S_LEN], BF16)
    vd_s = scr("vd_s", [S_LEN, 256], BF16)
    x1 = scr("x1", [HALF, D])
    x1T = scr("x1T", [D, HALF], BF16)
    x2 = scr("x2", [HALF, D])
    x2T = scr("x2T", [D, HALF], BF16)
    x2T_g = scr("x2T_g", [2 * D, HALF], BF16)
    qkT_s = scr("qkT_s", [1024, S_LEN], BF16)
    v_s = scr("v_s", [S_LEN, 512], BF16)
    o_s = scr("o_s", [S_LEN, 512])
    gi_s = scr("gi_s", [2, S_LEN])
    gf_s = scr("gf_s", [2, S_LEN])
    cols_s = scr("cols_s", [4, 128, 64])
    x3 = scr("x3", [HALF, D])
    x3T = scr("x3T", [D, HALF], BF16)
    x4T = scr("x4T", [D, HALF], BF16)
    xhT0 = scr("xhT0", [D, 128], BF16)
    xhT1 = scr("xhT1", [D, 128], BF16)
    x2h = scr("x2h", [128, D])
    x2h_g = scr("x2h_g", [256, D])

    NP = int(os.environ.get("NPHASE", "99"))
    phases = [
        lambda: phase_l0a(nc, x, w_in0, rot, ident, mask, retg, yT_loc, qkdT_s, vd_s),
        lambda: phase_l0b(nc, qkdT_s, vd_s, bm, yT_loc),
        lambda: phase_allgather(nc, yT_loc, yT_g, 512),
        lambda: phase_outproj(nc, "l0c", xh, yT_g, sel, w_out0, ln_g[0, 0], ln_b[0, 0], ident, x1, x1T, halo_x=x[HALF - 128:HALF, :], xhT_out=xhT0),
        lambda: phase_ffn(nc, "l0d", x1, x1T, w_up[0], cw[0], cbv[0], w_dn[0], ln_g[0, 1], ln_b[0, 1], ident, x2, x2T, ntiles=16, xhT_in=xhT0, sel_d=sel),
        lambda: phase_allgather(nc, x2T, x2T_g, 1024),
        lambda: phase_allgather_small(nc, x2[HALF - 128:HALF, :], x2h_g),
        lambda: phase_l1a(nc, x2T_g, w_in1, cwm, qkT_s, v_s, o_s, gi_s, gf_s),
        lambda: phase_l1p(nc, gi_s, gf_s, gb, ident, e127, cols_s),
        lambda: phase_l1b(nc, qkT_s, v_s, o_s, cols_s, ident, mask, yT_loc),
        lambda: phase_allgather(nc, yT_loc, yT_g, 512),
        lambda: phase_outproj(nc, "l1c", x2, yT_g, sel, w_out1, ln_g[1, 0], ln_b[1, 0], ident, x3, x3T, halo_x=x2h_g[0:128, :], xhT_out=xhT1),
        lambda: phase_ffn(nc, "l1d", x3, x3T, w_up[1], cw[1], cbv[1], w_dn[1], ln_g[1, 1], ln_b[1, 1], ident, out, x4T, ntiles=16, xhT_in=xhT1, sel_d=sel),
    ]
    for ph in phases[:NP]:
        ph()
    return nc


def kernel(x, even_w_in, even_w_out, rel_bias, odd_w_in, odd_gate_b, odd_conv_w, odd_w_out,
           ffn_w_up, ffn_conv_w, ffn_conv_b, ffn_w_down, ln_g, ln_b):
    f = lambda a: np.ascontiguousarray(np.asarray(a, dtype=np.float32))
    c = make_consts()
    e127 = np.zeros((128, 128), np.float32)
    e127[127, :] = 1.0
    x = np.asarray(x, dtype=np.float32)
    nb = x.shape[0]
    w0 = np.asarray(even_w_in[0], dtype=np.float32)
    w1 = np.asarray(odd_w_in[0], dtype=np.float32)
    wo0 = np.asarray(even_w_out[0], dtype=np.float32)
    bm_all = make_bm(f(rel_bias))
    cwm_all = np.asarray(odd_conv_w[0], dtype=np.float32)
    gbv = np.asarray(odd_gate_b[0], dtype=np.float32)
    shared = dict(ident=c["ident"], mask=c["mask"], e127=e127, ln_g=f(ln_g), ln_b=f(ln_b), w_out1=f(odd_w_out[0]))
    for l in range(2):
        shared[f"w_up{l}"] = f(ffn_w_up[l])
        shared[f"w_dn{l}"] = f(ffn_w_down[l])
        shared[f"cw{l}"] = f(np.asarray(ffn_conv_w[l]).reshape(3, NFC, 128).transpose(2, 1, 0))
        shared[f"cb{l}"] = f(np.asarray(ffn_conv_b[l]).reshape(NFC, 128).T)
    per_m = []
    for m in range(2):
        r0, d0 = m * 256, m * 256
        cols0 = np.concatenate([np.arange(r0, r0 + 256), 512 + np.arange(r0, r0 + 256), 1024 + np.arange(r0, r0 + 256), 1536 + np.arange(r0, r0 + 256),
                                2048 + np.arange(d0, d0 + 256), 2560 + np.arange(d0, d0 + 256), 3072 + np.arange(d0, d0 + 256)])
        h0 = m * 512
        cols1 = np.concatenate([np.arange(h0, h0 + 512), 1024 + np.arange(h0, h0 + 512), 2048 + np.arange(h0, h0 + 512), 3072 + np.arange(h0, h0 + 512),
                                4096 + np.arange(2 * m, 2 * m + 2), 4100 + np.arange(2 * m, 2 * m + 2)])
        qkf = np.concatenate([np.arange(h0, h0 + 512), 1024 + np.arange(h0, h0 + 512)])
        retg = np.zeros((128, 2), np.float32)
        retg[:, 0] = RET_G[2 * m]
        retg[:, 1] = RET_G[2 * m + 1]
        sel = np.zeros((128, 2), np.float32)
        sel[:, m] = 1.0
        per_m.append(dict(
            w_in0=f(w0[:, cols0]), rot=f(c["rot"][:, :, m * 128:(m + 1) * 128]), retg=retg, sel=sel, bm=f(bm_all[4 * m:4 * m + 4]),
            w_in1=f(w1[:, cols1]), cwm=f(cwm_all[:, qkf].reshape(4, 8, 128).transpose(2, 1, 0)),
            gb=f(np.stack([gbv[2 * m:2 * m + 2], gbv[4 + 2 * m:4 + 2 * m + 2]], axis=1)),
        ))
    perm = np.concatenate([np.arange(0, 256), np.arange(512, 768), np.arange(256, 512), np.arange(768, 1024)])
    shared["w_out0"] = f(wo0)
    shared["w_out1"] = f(np.asarray(odd_w_out[0], dtype=np.float32)[perm])
    nc = build_program()
    in_maps = []
    for core in range(2 * nb):
        b, m = divmod(core, 2)
        in_maps.append(dict(shared, **per_m[m], x=np.ascontiguousarray(x[b]), xh=np.ascontiguousarray(x[b, m * HALF:(m + 1) * HALF])))
    res = run_bass_kernel_spmd(nc, in_maps, core_ids=list(range(2 * nb)))
    out = np.zeros((nb, S_LEN, D), np.float32)
    for core in range(2 * nb):
        b, m = divmod(core, 2)
        out[b, m * HALF:(m + 1) * HALF] = np.asarray(res.results[core]["out"], dtype=np.float32)
    return out
```
